# Optimizing a Trainium2 kernel written in Bass

```python
import math
import jax, jax.numpy as jnp
from jax import lax
import numpy as np

D_MODEL = 2048
BATCH = 2
SEQ = 8192
DEPTH = 4
DEC_BATCH = 16
DEC_SEQ = 2048
PAST_LEN = 128

N_MIXERS = 4
LAYERS_PER_MIXER = DEPTH // N_MIXERS
NORM_EPS = 1e-6

SSD_DI = 2 * D_MODEL
SSD_HEADDIM = 64
SSD_NH = SSD_DI // SSD_HEADDIM
SSD_G = 8
SSD_N = 128
SSD_CONV = 7
SSD_CHUNK = 128
SSD_CONV_CH = SSD_DI + 2 * SSD_G * SSD_N
SSD_IN = SSD_DI + SSD_CONV_CH + 2 * SSD_NH

ATT_H = 8
ATT_DH = D_MODEL // (2 * ATT_H)
ATT_VD = 2 * ATT_DH
ATT_QBLOCK = 128
ATT_IN = 2 * ATT_H * ATT_DH * 2 + ATT_H * ATT_VD
REL_BUCKETS = 32
REL_MAX_DIST = 128

ML_H = 8
ML_DQK = 128
ML_DV = 256
ML_CHUNK = 128
ML_QK = ML_H * ML_DQK
ML_V = ML_H * ML_DV
ML_IN = 2 * ML_QK + 2 * ML_V + 4 * ML_H

HG_H = 16
HG_DK = 128
HG_DV = 128
HG_CHUNK = 64
HG_K = HG_H * HG_DK
HG_V = HG_H * HG_DV
HG_IN = 3 * HG_K + 2 * HG_V

FFN_D = 5632
FFN_CONV = 3

kernel_name = "hybrid_bidir_ssd_diffattn_mlstm_hgrn2_encoder"


def _rmsnorm(x, w):
    xf = x.astype(jnp.float32)
    y = xf * lax.rsqrt(jnp.mean(xf * xf, axis=-1, keepdims=True) + NORM_EPS)
    return (y * w.astype(jnp.float32)).astype(x.dtype)


def _dwconv(x, w, b):
    width = w.shape[0]
    y = lax.conv_general_dilated(
        x, w[:, None, :].astype(x.dtype), window_strides=(1,),
        padding=[(width // 2, width // 2)],
        dimension_numbers=('NWC', 'WIO', 'NWC'),
        feature_group_count=x.shape[-1])
    return y + b.astype(x.dtype)


def _flip(t):
    return jnp.flip(t, axis=1)


def _to_chunks(t, n):
    return t.reshape(t.shape[0], t.shape[1] // n, n, *t.shape[2:])


def _ssd_scan(xdt, a, bm, cm):
    bsz, s = xdt.shape[:2]
    hg = SSD_NH // SSD_G
    L = SSD_CHUNK
    nc = s // L
    xc = _to_chunks(xdt, L).reshape(bsz, nc, L, SSD_G, hg, SSD_HEADDIM)
    ac = _to_chunks(a, L).reshape(bsz, nc, L, SSD_G, hg)
    bc = _to_chunks(bm, L)
    cc = _to_chunks(cm, L)
    a_cum = jnp.cumsum(ac, axis=2)
    causal = jnp.tril(jnp.ones((L, L), bool))
    seg = a_cum[:, :, :, None] - a_cum[:, :, None, :]
    decay_ts = jnp.exp(jnp.where(causal[None, None, :, :, None, None], seg, -jnp.inf))
    cb = jnp.einsum('bctgn,bcsgn->bctsg', cc, bc)
    y_diag = jnp.einsum('bctsg,bctsgh,bcsghp->bctghp', cb, decay_ts, xc)
    decay_end = jnp.exp(a_cum[:, :, -1:] - a_cum)
    states = jnp.einsum('bcsgn,bcsgh,bcsghp->bcghpn', bc, decay_end, xc)
    chunk_decay = jnp.exp(a_cum[:, :, -1])

    def step(h, inp):
        st, dec = inp
        return h * dec[..., None, None] + st, h

    h0 = jnp.zeros((bsz, SSD_G, hg, SSD_HEADDIM, SSD_N), jnp.float32)
    _, prev = lax.scan(step, h0, (jnp.moveaxis(states, 1, 0), jnp.moveaxis(chunk_decay, 1, 0)))
    prev = jnp.moveaxis(prev, 0, 1)
    y_off = jnp.einsum('bctgn,bctgh,bcghpn->bctghp', cc, jnp.exp(a_cum), prev)
    return (y_diag + y_off).reshape(bsz, s, SSD_NH, SSD_HEADDIM)


def _ssd_mixer(h, w_in, conv_w, conv_b, dt_bias, a_log, d_skip, norm_w, w_out):
    bsz, s, _ = h.shape
    f32 = jnp.float32
    proj = h @ w_in
    z, xbc, dt = jnp.split(proj, [SSD_DI, SSD_DI + SSD_CONV_CH], axis=-1)
    xbc = jax.nn.silu(_dwconv(xbc, conv_w, conv_b)).astype(f32)
    xs, bm, cm = jnp.split(xbc, [SSD_DI, SSD_DI + SSD_G * SSD_N], axis=-1)
    xs = xs.reshape(bsz, s, SSD_NH, SSD_HEADDIM)
    bm = bm.reshape(bsz, s, SSD_G, SSD_N)
    cm = cm.reshape(bsz, s, SSD_G, SSD_N)
    dt = jax.nn.softplus(dt.astype(f32).reshape(bsz, s, 2, SSD_NH) + dt_bias.astype(f32))
    a = -jnp.exp(a_log.astype(f32)) * dt
    y_fwd = _ssd_scan(xs * dt[:, :, 0, :, None], a[:, :, 0], bm, cm)
    y_bwd = _flip(_ssd_scan(_flip(xs * dt[:, :, 1, :, None]), _flip(a[:, :, 1]), _flip(bm), _flip(cm)))
    y = y_fwd + y_bwd + d_skip.astype(f32)[:, None] * xs
    y = y.reshape(bsz, s, SSD_DI) * jax.nn.silu(z.astype(f32))
    y = _rmsnorm(y, norm_w)
    return y.astype(h.dtype) @ w_out


def _rel_bucket(rel):
    half = REL_BUCKETS // 2
    max_exact = half // 2
    ret = jnp.where(rel > 0, half, 0)
    n = jnp.abs(rel)
    nf = jnp.maximum(n, 1).astype(jnp.float32)
    large = max_exact + (jnp.log(nf / max_exact) / math.log(REL_MAX_DIST / max_exact)
                         * (half - max_exact)).astype(jnp.int32)
    large = jnp.minimum(large, half - 1)
    return ret + jnp.where(n < max_exact, n, large)


def _diff_attn_mixer(h, w_qkv, q_norm, k_norm, lam_p, sub_norm, w_out, rel_bias, lambda_init):
    bsz, s, _ = h.shape
    f32 = jnp.float32
    qkv = h @ w_qkv
    q, k, v = jnp.split(qkv, [2 * ATT_H * ATT_DH, 4 * ATT_H * ATT_DH], axis=-1)
    q = _rmsnorm(q.reshape(bsz, s, ATT_H, 2, ATT_DH), q_norm)
    k = _rmsnorm(k.reshape(bsz, s, ATT_H, 2, ATT_DH), k_norm)
    v = v.reshape(bsz, s, ATT_H, ATT_VD)
    lp = lam_p.astype(f32)
    lam = jnp.exp(jnp.sum(lp[0] * lp[1])) - jnp.exp(jnp.sum(lp[2] * lp[3])) + lambda_init
    nb = s // ATT_QBLOCK
    q_blocks = jnp.moveaxis(q.reshape(bsz, nb, ATT_QBLOCK, ATT_H, 2, ATT_DH), 1, 0)
    k_pos = jnp.arange(s, dtype=jnp.int32)
    table = rel_bias.astype(f32)
    scale = ATT_DH ** -0.5

    def block(args):
        qb, bi = args
        q_pos = bi * ATT_QBLOCK + jnp.arange(ATT_QBLOCK, dtype=jnp.int32)
        bias = jnp.transpose(table[_rel_bucket(k_pos[None, :] - q_pos[:, None])], (2, 0, 1))
        logits = jnp.einsum('bqhjd,bkhjd->bjhqk', qb, k).astype(f32) * scale + bias[None, None]
        p = jax.nn.softmax(logits, axis=-1)
        attn = (p[:, 0] - lam * p[:, 1]).astype(v.dtype)
        return jnp.einsum('bhqk,bkhe->bqhe', attn, v)

    o = lax.map(block, (q_blocks, jnp.arange(nb, dtype=jnp.int32)))
    o = jnp.moveaxis(o, 0, 1).reshape(bsz, s, ATT_H, ATT_VD)
    o = _rmsnorm(o, sub_norm) * (1.0 - lambda_init)
    return o.reshape(bsz, s, ATT_H * ATT_VD).astype(h.dtype) @ w_out


def _mlstm_scan(q, k, v, ig, fg):
    bsz, s = q.shape[:2]
    L = ML_CHUNK
    qc, kc, vc = _to_chunks(q, L), _to_chunks(k, L), _to_chunks(v, L)
    igc = _to_chunks(ig, L)
    bcum = jnp.cumsum(_to_chunks(jax.nn.log_sigmoid(fg), L), axis=2)
    g_end = bcum[:, :, -1]
    causal = jnp.tril(jnp.ones((L, L), bool))
    d_ts = jnp.where(causal[None, None, :, :, None],
                     bcum[:, :, :, None] - bcum[:, :, None, :] + igc[:, :, None, :], -jnp.inf)
    w_s = g_end[:, :, None] - bcum + igc

    def step(carry, inp):
        c_st, n_st, m_st = carry
        kk, vv, ws, ge = inp
        m_new = jnp.maximum(ge + m_st, jnp.max(ws, axis=1))
        decay = jnp.exp(ge + m_st - m_new)
        wts = jnp.exp(ws - m_new[:, None])
        c_new = decay[..., None, None] * c_st + jnp.einsum('bsh,bshk,bshv->bhkv', wts, kk, vv)
        n_new = decay[..., None] * n_st + jnp.einsum('bsh,bshk->bhk', wts, kk)
        return (c_new, n_new, m_new), (c_st, n_st, m_st)

    init = (jnp.zeros((bsz, ML_H, ML_DQK, ML_DV), jnp.float32),
            jnp.zeros((bsz, ML_H, ML_DQK), jnp.float32),
            jnp.zeros((bsz, ML_H), jnp.float32))
    xs = (jnp.moveaxis(kc, 1, 0), jnp.moveaxis(vc, 1, 0), jnp.moveaxis(w_s, 1, 0), jnp.moveaxis(g_end, 1, 0))
    _, (c_prev, n_prev, m_prev) = lax.scan(step, init, xs)
    c_prev = jnp.moveaxis(c_prev, 0, 1)
    n_prev = jnp.moveaxis(n_prev, 0, 1)
    m_prev = jnp.moveaxis(m_prev, 0, 1)
    a_t = bcum + m_prev[:, :, None]
    m_t = jnp.maximum(a_t, jnp.max(d_ts, axis=3))
    scores = jnp.einsum('bcthk,bcshk->bctsh', qc, kc) * jnp.exp(d_ts - m_t[:, :, :, None])
    inter = jnp.exp(a_t - m_t)
    num = (jnp.einsum('bctsh,bcshv->bcthv', scores, vc)
           + inter[..., None] * jnp.einsum('bcthk,bchkv->bcthv', qc, c_prev))
    den = jnp.sum(scores, axis=3) + inter * jnp.einsum('bcthk,bchk->bcth', qc, n_prev)
    out = num / jnp.maximum(jnp.abs(den), jnp.exp(-m_t))[..., None]
    return out.reshape(bsz, s, ML_H, ML_DV)


def _mlstm_mixer(h, w_in, gate_b, norm_w, w_out):
    bsz, s, _ = h.shape
    proj = (h @ w_in).astype(jnp.float32)
    q, k, v, o, gates = jnp.split(proj, [ML_QK, 2 * ML_QK, 2 * ML_QK + ML_V, 2 * ML_QK + 2 * ML_V], axis=-1)
    q = q.reshape(bsz, s, ML_H, ML_DQK)
    k = k.reshape(bsz, s, ML_H, ML_DQK) * (ML_DQK ** -0.5)
    v = v.reshape(bsz, s, ML_H, ML_DV)
    gates = gates.reshape(bsz, s, 4, ML_H) + gate_b.astype(jnp.float32)
    h_fwd = _mlstm_scan(q, k, v, gates[:, :, 0], gates[:, :, 1])
    h_bwd = _flip(_mlstm_scan(_flip(q), _flip(k), _flip(v), _flip(gates[:, :, 2]), _flip(gates[:, :, 3])))
    y = _rmsnorm(h_fwd + h_bwd, norm_w.reshape(ML_H, ML_DV)).reshape(bsz, s, ML_V)
    y = y * jax.nn.sigmoid(o)
    return y.astype(h.dtype) @ w_out


def _hgrn_scan(q, k, logf, v):
    bsz, s = q.shape[:2]
    L = HG_CHUNK
    qc, kc, fc, vc = _to_chunks(q, L), _to_chunks(k, L), _to_chunks(logf, L), _to_chunks(v, L)
    bcum = jnp.cumsum(fc, axis=2)
    b_end = bcum[:, :, -1]
    q_dec = qc * jnp.exp(bcum)
    k_inv = kc * jnp.exp(-bcum)
    causal = jnp.tril(jnp.ones((L, L), bool))
    scores = jnp.where(causal[None, None, None], jnp.einsum('bcthk,bcshk->bchts', q_dec, k_inv), 0.0)
    intra = jnp.einsum('bchts,bcshv->bcthv', scores, vc)
    k_end = kc * jnp.exp(b_end[:, :, None] - bcum)
    dstate = jnp.einsum('bcshk,bcshv->bchkv', k_end, vc)

    def step(st, inp):
        ds, be = inp
        return jnp.exp(be)[..., None] * st + ds, st

    s0 = jnp.zeros((bsz, HG_H, HG_DK, HG_DV), jnp.float32)
    _, s_prev = lax.scan(step, s0, (jnp.moveaxis(dstate, 1, 0), jnp.moveaxis(b_end, 1, 0)))
    s_prev = jnp.moveaxis(s_prev, 0, 1)
    inter = jnp.einsum('bcthk,bchkv->bcthv', q_dec, s_prev)
    return (intra + inter).reshape(bsz, s, HG_H, HG_DV)


def _hgrn_mixer(h, w_in, lb, norm_w, w_out):
    bsz, s, _ = h.shape
    proj = (h @ w_in).astype(jnp.float32)
    q, f_fw, f_bw, i_in, g = jnp.split(proj, [HG_K, 2 * HG_K, 3 * HG_K, 3 * HG_K + HG_V], axis=-1)
    f_f = (lb + (1.0 - lb) * jax.nn.sigmoid(f_fw)).reshape(bsz, s, HG_H, HG_DK)
    f_b = (lb + (1.0 - lb) * jax.nn.sigmoid(f_bw)).reshape(bsz, s, HG_H, HG_DK)
    q = q.reshape(bsz, s, HG_H, HG_DK)
    v = i_in.reshape(bsz, s, HG_H, HG_DV)
    o_fwd = _hgrn_scan(q, 1.0 - f_f, jnp.log(f_f), v)
    o_bwd = _flip(_hgrn_scan(_flip(q), _flip(1.0 - f_b), _flip(jnp.log(f_b)), _flip(v)))
    y = _rmsnorm(o_fwd + o_bwd, norm_w.reshape(HG_H, HG_DV)).reshape(bsz, s, HG_V)
    y = y * jax.nn.silu(g)
    return y.astype(h.dtype) @ w_out


def _conv_ffn(h, w_in, conv_w, conv_b, w_out):
    u = _dwconv(h @ w_in, conv_w, conv_b)
    a, b = jnp.split(u, 2, axis=-1)
    return (jax.nn.silu(a) * b) @ w_out


def _trunk(x, p):
    lbs = jnp.cumsum(jax.nn.softmax(p['hgrn_lb'].astype(jnp.float32), axis=0), axis=0)
    for i in range(DEPTH):
        j = i // N_MIXERS
        kind = i % N_MIXERS
        h = _rmsnorm(x, p['ln1'][i])
        if kind == 0:
            m = _ssd_mixer(h, p['ssd_w_in'][j], p['ssd_conv_w'][j], p['ssd_conv_b'][j], p['ssd_dt_bias'][j],
                           p['ssd_a_log'][j], p['ssd_d'][j], p['ssd_norm_w'][j], p['ssd_w_out'][j])
        elif kind == 1:
            m = _diff_attn_mixer(h, p['attn_w_qkv'][j], p['attn_q_norm'][j], p['attn_k_norm'][j],
                                 p['attn_lambda'][j], p['attn_sub_norm'][j], p['attn_w_out'][j],
                                 p['rel_bias'], 0.8 - 0.6 * math.exp(-0.3 * i))
        elif kind == 2:
            m = _mlstm_mixer(h, p['mlstm_w_in'][j], p['mlstm_gate_b'][j], p['mlstm_norm_w'][j], p['mlstm_w_out'][j])
        else:
            m = _hgrn_mixer(h, p['hgrn_w_in'][j], lbs[i] - lbs[0], p['hgrn_norm_w'][j], p['hgrn_w_out'][j])
        x = x + m.astype(x.dtype)
        h = _rmsnorm(x, p['ln2'][i])
        x = x + _conv_ffn(h, p['ffn_w_in'][i], p['ffn_conv_w'][i], p['ffn_conv_b'][i], p['ffn_w_out'][i]).astype(x.dtype)
    return x


def setup_inputs(seed: int = 0) -> dict:
    key = jax.random.key(seed)
    ks = iter(jax.random.split(key, 64))
    f32 = jnp.float32

    def nrm(shape, scale):
        return scale * jax.random.normal(next(ks), shape, f32)

    L = LAYERS_PER_MIXER
    out_scale = 0.5
    dt = jnp.exp(jax.random.uniform(next(ks), (L, 2, SSD_NH), f32)
                 * (math.log(0.1) - math.log(0.001)) + math.log(0.001))
    f_base = jnp.linspace(3.0, 6.0, ML_H, dtype=f32)
    z_base = jnp.zeros((ML_H,), f32)
    gate_base = jnp.stack([z_base, f_base, z_base, f_base])[None]
    return {
        "x_prompt": nrm((BATCH, SEQ, D_MODEL), 1.0),
        "x_sample": nrm((DEC_BATCH, DEC_SEQ, D_MODEL), 1.0),
        "ln1": 1.0 + nrm((DEPTH, D_MODEL), 0.02),
        "ln2": 1.0 + nrm((DEPTH, D_MODEL), 0.02),
        "ssd_w_in": nrm((L, D_MODEL, SSD_IN), D_MODEL ** -0.5),
        "ssd_conv_w": nrm((L, SSD_CONV, SSD_CONV_CH), SSD_CONV ** -0.5),
        "ssd_conv_b": nrm((L, SSD_CONV_CH), 0.01),
        "ssd_dt_bias": dt + jnp.log(-jnp.expm1(-dt)),
        "ssd_a_log": jnp.log(jax.random.uniform(next(ks), (L, 2, SSD_NH), f32, minval=1.0, maxval=16.0)),
        "ssd_d": 1.0 + nrm((L, SSD_NH), 0.02),
        "ssd_norm_w": 1.0 + nrm((L, SSD_DI), 0.02),
        "ssd_w_out": nrm((L, SSD_DI, D_MODEL), out_scale * SSD_DI ** -0.5),
        "attn_w_qkv": nrm((L, D_MODEL, ATT_IN), D_MODEL ** -0.5),
        "attn_q_norm": 1.0 + nrm((L, ATT_DH), 0.02),
        "attn_k_norm": 1.0 + nrm((L, ATT_DH), 0.02),
        "attn_lambda": nrm((L, 4, ATT_DH), 0.1),
        "attn_sub_norm": 1.0 + nrm((L, ATT_VD), 0.02),
        "attn_w_out": nrm((L, ATT_H * ATT_VD, D_MODEL), out_scale * (ATT_H * ATT_VD) ** -0.5),
        "rel_bias": nrm((REL_BUCKETS, ATT_H), 0.5),
        "mlstm_w_in": nrm((L, D_MODEL, ML_IN), D_MODEL ** -0.5),
        "mlstm_gate_b": gate_base + nrm((L, 4, ML_H), 0.1),
        "mlstm_norm_w": 1.0 + nrm((L, ML_V), 0.02),
        "mlstm_w_out": nrm((L, ML_V, D_MODEL), out_scale * ML_V ** -0.5),
        "hgrn_w_in": nrm((L, D_MODEL, HG_IN), D_MODEL ** -0.5),
        "hgrn_lb": nrm((DEPTH, HG_K), 0.1),
        "hgrn_norm_w": 1.0 + nrm((L, HG_V), 0.02),
        "hgrn_w_out": nrm((L, HG_V, D_MODEL), out_scale * HG_V ** -0.5),
        "ffn_w_in": nrm((DEPTH, D_MODEL, 2 * FFN_D), D_MODEL ** -0.5),
        "ffn_conv_w": nrm((DEPTH, FFN_CONV, 2 * FFN_D), FFN_CONV ** -0.5),
        "ffn_conv_b": nrm((DEPTH, 2 * FFN_D), 0.01),
        "ffn_w_out": nrm((DEPTH, FFN_D, D_MODEL), out_scale * FFN_D ** -0.5),
    }


def reference(x_prompt, x_sample, ln1, ln2,
              ssd_w_in, ssd_conv_w, ssd_conv_b, ssd_dt_bias, ssd_a_log, ssd_d, ssd_norm_w, ssd_w_out,
              attn_w_qkv, attn_q_norm, attn_k_norm, attn_lambda, attn_sub_norm, attn_w_out, rel_bias,
              mlstm_w_in, mlstm_gate_b, mlstm_norm_w, mlstm_w_out,
              hgrn_w_in, hgrn_lb, hgrn_norm_w, hgrn_w_out,
              ffn_w_in, ffn_conv_w, ffn_conv_b, ffn_w_out):
    p = dict(ln1=ln1, ln2=ln2,
             ssd_w_in=ssd_w_in, ssd_conv_w=ssd_conv_w, ssd_conv_b=ssd_conv_b, ssd_dt_bias=ssd_dt_bias,
             ssd_a_log=ssd_a_log, ssd_d=ssd_d, ssd_norm_w=ssd_norm_w, ssd_w_out=ssd_w_out,
             attn_w_qkv=attn_w_qkv, attn_q_norm=attn_q_norm, attn_k_norm=attn_k_norm,
             attn_lambda=attn_lambda, attn_sub_norm=attn_sub_norm, attn_w_out=attn_w_out, rel_bias=rel_bias,
             mlstm_w_in=mlstm_w_in, mlstm_gate_b=mlstm_gate_b, mlstm_norm_w=mlstm_norm_w, mlstm_w_out=mlstm_w_out,
             hgrn_w_in=hgrn_w_in, hgrn_lb=hgrn_lb, hgrn_norm_w=hgrn_norm_w, hgrn_w_out=hgrn_w_out,
             ffn_w_in=ffn_w_in, ffn_conv_w=ffn_conv_w, ffn_conv_b=ffn_conv_b, ffn_w_out=ffn_w_out)
    y_prompt = _trunk(x_prompt, p)
    y_sample = _trunk(x_sample, p)
    return (y_prompt, y_sample)
```

```python
import math
from contextlib import ExitStack
import numpy as np
import concourse.bass as bass
import concourse.mybir as mybir
from concourse.bass_utils import run_bass_kernel_spmd

F32 = mybir.dt.float32
BF16 = mybir.dt.bfloat16
AF = mybir.ActivationFunctionType
ALU = mybir.AluOpType
AX = mybir.AxisListType

D = 2048
KC = D // 128
FFN_D = 5632
EPS = 1e-6
TB = 512
SAME_ENG_SYNC = True


class Buf:
    __slots__ = ("name", "w", "r")

    def __init__(self, name):
        self.name = name
        self.w = None
        self.r = {}


class Sched:
    NDMA = 12

    def __init__(self, nc, es):
        self.nc = nc
        self.eng = {"pe": nc.tensor, "act": nc.scalar, "dve": nc.vector, "pool": nc.gpsimd, "sp": nc.sync}
        self.sem = {k: es.enter_context(nc.semaphore("s_" + k)) for k in self.eng}
        self.cnt = {k: 0 for k in self.eng}
        self.seen = {k: {} for k in self.eng}
        self.dsem = {q: [es.enter_context(nc.semaphore(f"d_{q}{i}")) for i in range(self.NDMA)] for q in ("sp", "pool", "act")}
        self.dcnt = {q: [0] * self.NDMA for q in self.dsem}
        self.drr = {q: 0 for q in self.dsem}
        self.semobj = {}
        self.nwait = 0

    def _wait(self, e, tok):
        if tok is None:
            return
        key, val, semh, owner = tok
        if owner == e and not (SAME_ENG_SYNC and e in ("act", "dve", "pool")):
            return
        if self.seen[e].get(key, 0) >= val:
            return
        self.seen[e][key] = val
        self.eng[e].wait_ge(semh, val)
        self.nwait += 1

    def _deps(self, e, r, w):
        for b in r:
            self._wait(e, b.w)
        for b in w:
            self._wait(e, b.w)
            for t in b.r.values():
                self._wait(e, t)

    def _mark(self, tok, r, w):
        for b in r:
            b.r[tok[0]] = tok
        for b in w:
            b.w = tok
            b.r = {}

    def op(self, e, fn, r=(), w=()):
        self._deps(e, r, w)
        ins = fn(self.eng[e])
        self.cnt[e] += 1
        ins.then_inc(self.sem[e], 1)
        tok = (e, self.cnt[e], self.sem[e], e)
        self._mark(tok, r, w)
        return tok

    def dma(self, q, out, in_, r=(), w=(), **kw):
        self._deps(q, r, w)
        k = self.drr[q]
        self.drr[q] = (k + 1) % self.NDMA
        key = f"d_{q}{k}"
        semh = self.dsem[q][k]
        if self.dcnt[q][k]:
            self._wait(q, (key, self.dcnt[q][k], semh, None))
        self.dcnt[q][k] += 16
        self.eng[q].dma_start(out=out, in_=in_, **kw).then_inc(semh, 16)
        tok = (key, self.dcnt[q][k], semh, None)
        self._mark(tok, r, w)
        return tok

    def finish(self, bufs):
        for b in bufs:
            self._wait("sp", b.w)


class Ctx:
    pass


def build(NSLOT, SL, layers=(0, 1, 2, 3), mixers=True, ffn=True, debug=(), impl=(0, 1, 2, 3)):
    NT = NSLOT * SL // 128
    TOK = NSLOT * SL
    NBLK = TOK // TB
    nc = bass.Bass("TRN2", target_bir_lowering=False)
    es = ExitStack()
    S = Sched(nc, es)
    g = Ctx()

    def din(name, shape):
        return nc.dram_tensor(name, list(shape), F32, kind="ExternalInput").ap()

    def dscr(name, shape, dt=F32):
        if name in debug:
            return nc.dram_tensor(name, list(shape), dt, kind="ExternalOutput").ap()
        return nc.dram_tensor(name, list(shape), dt).ap()

    def sb(name, shape, dt=F32):
        t = es.enter_context(nc.sbuf_tensor("sb_" + name, list(shape), dt))
        return t, Buf(name)

    x_in = din("x", [TOK, D])
    link_in = din("link", [128, 1])
    y_out = nc.dram_tensor("y", [TOK, D], F32, kind="ExternalOutput").ap()
    ident_in = din("ident", [128, 128])
    ln1 = din("ln1", [4, D])
    ln2 = din("ln2", [4, D])
    if ffn:
        ffn_w_in = din("ffn_w_in", [4, D, 2 * FFN_D])
        ffn_conv_w = din("ffn_conv_w", [4, 3, 2 * FFN_D])
        ffn_conv_b = din("ffn_conv_b", [4, 2 * FFN_D])
        ffn_w_out = din("ffn_w_out", [4, FFN_D, D])

    xT2 = [dscr(f"xT{i}", [D, TOK]) for i in range(2)]
    xT_v2 = [t.rearrange("(c p) t -> p c t", p=128) for t in xT2]
    bxT2 = [[Buf(f"xT{i}_{b}") for b in range(NBLK)] for i in range(2)]
    g.cur = 0
    ffn_wi_b = [dscr(f"ffn_wi_b{l}", [2 * FFN_D // 256, 128, KC, 256], BF16) for l in range(4)]
    ffn_wo_b = [dscr(f"ffn_wo_b{l}", [D // 128, 128, FFN_D // 128, 128], BF16) for l in range(4)]
    bW = Buf("weights_bf16")

    MIX = set(l % 4 for l in layers if l % 4 in impl) if mixers else set()
    yT_d = dscr("yT_d", [4096, TOK], BF16)
    yT_v = yT_d.rearrange("(c p) t -> p c t", p=128)
    byT = [Buf(f"yT{b}") for b in range(NBLK)]

    RECUR = bool(MIX & {0, 2, 3})
    if RECUR:
        masks_in = din("masks", [2, 128, 128])
        qd_d = dscr("qd_d", [2, 16, 128, TOK], BF16)
        ki_d = dscr("ki_d", [2, 16, 128, TOK], BF16)
        kitm_d = dscr("kitm_d", [2, 16, TOK, 128], BF16)
        eb_d = dscr("eb_d", [2, 16, 128, NT])
        g_d = dscr("g_d", [16, 128, TOK], BF16)
        oT_d = dscr("oT_d", [16, 128, TOK])
        vr_d = dscr("vr_d", [TOK, D], BF16)
        b_qd, b_ki, b_kitm, b_eb, b_gd, b_oT, b_vr = (Buf(n) for n in ("qd", "ki", "kitm", "eb", "gd", "oT", "vr"))
    if 3 in MIX:
        hgrn_w_in = din("hgrn_w_in", [1, D, 10240])
        hgrn_lb = din("hgrn_lb", [4, D])
        hgrn_norm_w = din("hgrn_norm_w", [1, D])
        hgrn_w_out = din("hgrn_w_out", [1, D, D])
        hgrn_wi_b = dscr("hgrn_wi_b", [40, 128, KC, 256], BF16)
        hgrn_wo_b = dscr("hgrn_wo_b", [16, 128, KC, 128], BF16)

    if 2 in MIX:
        mlstm_w_in = din("mlstm_w_in", [1, D, 6176])
        mlstm_gate_b = din("mlstm_gate_b", [1, 4, 8])
        mlstm_norm_w = din("mlstm_norm_w", [1, D])
        mlstm_w_out = din("mlstm_w_out", [1, D, D])
        mlstm_wi_b = dscr("mlstm_wi_b", [25, 128, KC, 256], BF16)
        mlstm_wo_b = dscr("mlstm_wo_b", [16, 128, KC, 128], BF16)
    if MIX & {0, 2}:
        k2_d = dscr("k2_d", [TOK, 1024], BF16)
        og_d = dscr("og_d", [TOK, 4096], BF16)
        gates_d = dscr("gates_d", [TOK, 128])
        hfw_d = dscr("hfw_d", [TOK, 4096])
        b_k2, b_og, b_gates, b_hfw = (Buf(n) for n in ("k2", "og", "gates", "hfw"))

    if 0 in MIX:
        ssd_w_in = din("ssd_w_in", [1, D, 10368])
        ssd_conv_w = din("ssd_conv_w", [1, 7, 6144])
        ssd_conv_b = din("ssd_conv_b", [1, 6144])
        ssd_dt_bias = din("ssd_dt_bias", [1, 2, 64])
        ssd_a_log = din("ssd_a_log", [1, 2, 64])
        ssd_dsk = din("ssd_d", [1, 64])
        ssd_norm_w = din("ssd_norm_w", [1, 4096])
        ssd_w_out = din("ssd_w_out", [1, 4096, D])
        ssd_wi_b = dscr("ssd_wi_b", [41, 128, KC, 256], BF16)
        ssd_wo_b = dscr("ssd_wo_b", [16, 128, 32, 128], BF16)
        xs_d = dscr("xs_d", [TOK, 4096], BF16)
        btm_d = dscr("btm_d", [TOK, 1024], BF16)
        bT_d = dscr("bT_d", [8, 128, TOK], BF16)
        cT_d = dscr("cT_d", [8, 128, TOK], BF16)
        dt_d = dscr("dt_d", [TOK, 128])
        b_xsd, b_btm, b_bT, b_cT, b_dt = (Buf(n) for n in ("xs_d", "btm_d", "bT_d", "cT_d", "dt_d"))
    if 1 in MIX:
        attn_w_qkv = din("attn_w_qkv", [1, D, 6144])
        attn_q_norm = din("attn_q_norm", [1, 128])
        attn_k_norm = din("attn_k_norm", [1, 128])
        attn_lambda = din("attn_lambda", [1, 4, 128])
        attn_sub_norm = din("attn_sub_norm", [1, 256])
        attn_w_out = din("attn_w_out", [1, D, D])
        rel_bias = din("rel_bias", [32, 8])
        oh_in = din("oh", [32, 128, 1152])
        attn_wqkv_b = dscr("attn_wqkv_b", [24, 128, KC, 256], BF16)
        attn_wo_b = dscr("attn_wo_b", [16, 128, KC, 128], BF16)
        qT_d = dscr("qT_d", [16, 128, TOK], BF16)
        kT_d = dscr("kT_d", [16, 128, TOK], BF16)
        v_d = dscr("v_d", [TOK, D], BF16)
        T_d = dscr("T_d", [8, 128, 1152], BF16)
        b_qT, b_kT, b_vd, b_Td = Buf("qT_d"), Buf("kT_d"), Buf("v_d"), Buf("T_d")

    ident, b_ident = sb("ident", [128, 128])
    identb, b_identb = sb("identb", [128, 128], BF16)
    ones, b_ones = sb("ones", [128, 128])
    linkt, b_link = sb("linkt", [128, 1])
    epst, b_eps = sb("epst", [128, 1])
    onesD, b_onesD = sb("onesD", [128, 128])
    HMAX = 3
    xs, b_xs = sb("xs", [128, KC, TB + 2 * HMAX])
    hT, b_hT = sb("hT", [128, KC, TB + 2 * HMAX], BF16)
    sq = [sb(f"sq{i}", [128, TB + 2 * HMAX]) for i in range(2)]
    rstd, b_rstd = sb("rstd", [128, TB + 2 * HMAX])
    lnc, b_lnc = sb("lnc", [128, 8, KC])
    wp = [sb(f"wp{i}", [128, KC, 256], BF16) for i in range(2)]
    wo = [sb(f"wo{i}", [128, 44, 128], BF16) for i in range(2)]
    actT, b_actT = sb("actT", [128, 44, TB], BF16)
    cst = [sb(f"cst{i}", [128, 512]) for i in range(2)]
    fcw, b_fcw = sb("fcw", [128, 4, 88])
    ua, b_ua = sb("ua", [128, TB + 6])
    ub, b_ub = sb("ub", [128, TB + 2])
    acca, b_acca = sb("acca", [128, TB])
    accb, b_accb = sb("accb", [128, TB])
    sa, b_sa = sb("sa", [128, TB])
    xr = [sb(f"xr{i}", [128, TB]) for i in range(2)]
    vrow, b_vrow = sb("vrow", [128, 128])
    ones128, b_ones128 = sb("ones128", [128, 128])
    rowt, b_rowt = sb("rowt", [1, 512])
    qkn, b_qkn = sb("qkn", [128, 2])
    lamt, b_lam = sb("lamt", [128, 8])
    subw, b_subw = sb("subw", [128, 256])
    tbb, b_tbb = sb("tbb", [128, 256])
    bcol, b_bcol = sb("bcol", [128, 40])
    kT2, b_kT2 = sb("kT2", [128, max(TOK, 8192)], BF16)
    Tt, b_Tt = sb("Tt", [128, 1152], BF16)
    xsflat = xs[:, :, :].rearrange("p a b -> p (a b)")
    Tacc, b_Tacc = xsflat[:, 0:1152], b_xs
    pT = [sb(f"pT{i}", [128, 512], BF16) for i in range(2)]
    qblk = [sb(f"qblk{i}", [128, 512], BF16) for i in range(2)]
    Ost = [(xsflat[:, 1152 + i * 1028:1152 + (i + 1) * 1028].rearrange("p (q e) -> p q e", e=257), b_xs) for i in range(2)]
    osm, b_osm = sb("osm", [128, 32])
    ot, b_ot = acca[:, 0:256], b_acca
    ot2, b_ot2 = accb[:, 0:256], b_accb
    ytb_, b_ytb = sb("ytb", [128, 2, 512], BF16)
    st16 = [sb(f"st16_{i}", [128, 512], BF16) for i in range(2)]
    maskt = [sb(f"mask{i}", [128, 128]) for i in range(2)]
    ones512, b_ones512 = sb("ones512", [128, 512])
    pcol, b_pcol = sb("pcol", [128, 8, 16])
    ebt, b_ebt = sb("ebt", [128, TOK // 128])
    ebs, b_ebs = sb("ebs", [128, 8])
    S32, b_S32 = sb("S32", [128, 512])
    Sb, b_Sb = sb("Sb", [128, 512], BF16)
    at_ = [sb(f"at{i}", [128, 128], BF16) for i in range(2)]
    kst, b_kst = sb("kst", [128, 4, 128], BF16)
    w32 = [(acca, b_acca), (accb, b_accb), (sa, b_sa), (ub, b_ub)]
    negm = [sb(f"negm{i}", [128, 128]) for i in range(2)]
    abc, b_abc = sb("abc", [128, 128])
    Et = [sb(f"Et{i}", [128, 128], BF16) for i in range(2)]
    kwt, b_kwt = sb("kwt", [128, 128], BF16)
    ktc = [sb(f"ktc{i}", [128, 128], BF16) for i in range(2)]
    o32, b_o32 = xr[0]
    gbc, b_gbc = sb("gbc", [128, 128])
    nwbc, b_nwbc = xr[1]
    ccw, b_ccw = sb("ccw", [128, 8, 48])
    u7, b_u7 = ua, b_ua
    TABc, b_TABc = sb("TABc", [128, 5, 64])
    Dbc, b_Dbc = sb("Dbc", [128, 64])
    dtb, b_dtb = gbc, b_gbc
    nga, b_nga = sb("nga", [128, 128])
    nwc, b_nwc = sb("nwc", [128, 32])
    banks = []
    for i in range(8):
        t = es.enter_context(nc.psum_tensor(f"bank{i}", [128, 512], F32))
        banks.append((t, Buf(f"bank{i}")))
    g.bank_rr = 0
    g.bank_pool = list(range(8))

    def bank():
        t = banks[g.bank_pool[g.bank_rr % len(g.bank_pool)]]
        g.bank_rr += 1
        return t

    def bcast_row(dst_ap, dst_b, vec_ap, n):
        S.dma("sp", rowt[0:1, 0:n], vec_ap, w=[b_rowt])
        pt, pb = bank()
        mm(pt[:, 0:n], pb, ones[0:1, :], b_ones, rowt[0:1, 0:n], b_rowt, True, True)
        evac(dst_ap, dst_b, pt[:, 0:n], pb, eng="dve")

    def mm(out_ap, out_b, l_ap, l_b, r_ap, r_b, start, stop):
        S.op("pe", lambda e: e.matmul(out_ap, lhsT=l_ap, rhs=r_ap, start=start, stop=stop), r=[l_b, r_b], w=[out_b])

    def tr(out_ap, out_b, in_ap, in_b, idn=None, idn_b=None):
        idn = ident if idn is None else idn
        idn_b = b_ident if idn_b is None else idn_b
        n = in_ap.shape[0]
        S.op("pe", lambda e: e.transpose(out_ap, in_ap, idn[0:n, 0:n]), r=[in_b, idn_b], w=[out_b])

    g.evac_rr = 0

    def evac(out_ap, out_b, in_ap, in_b, eng=None):
        if eng is None:
            eng = ("act", "dve")[g.evac_rr % 2]
            g.evac_rr += 1
        if eng == "act":
            S.op("act", lambda e: e.copy(out=out_ap, in_=in_ap), r=[in_b], w=[out_b])
        else:
            S.op(eng, lambda e: e.tensor_copy(out=out_ap, in_=in_ap), r=[in_b], w=[out_b])

    def load_cols(dst_ap, dst_b, vec_ap, C):
        S.dma("sp", vrow[0:C, :], vec_ap.rearrange("(c p) -> c p", p=128), w=[b_vrow])
        pt, pb = bank()
        tr(pt[:, 0:C], pb, vrow[0:C, :], b_vrow)
        evac(dst_ap, dst_b, pt[:, 0:C], pb, eng="dve")

    S.dma("sp", ident[:], ident_in[:, :], w=[b_ident])
    S.dma("sp", linkt[:], link_in[:, :], w=[b_link])
    S.op("dve", lambda e: e.memset(ones[:], 1.0), w=[b_ones])
    S.op("dve", lambda e: e.memset(epst[:], EPS), w=[b_eps])
    S.op("dve", lambda e: e.memset(ones128[:], 1.0 / 128), w=[b_ones128])
    S.op("dve", lambda e: e.memset(ones512[:], 1.0), w=[b_ones512])
    if RECUR:
        for i in range(2):
            S.dma("sp", maskt[i][0][:, :], masks_in[i], w=[maskt[i][1]])
            S.op("dve", lambda e: e.tensor_scalar(out=negm[i][0][:, :], in0=maskt[i][0][:, :], scalar1=-1.0, scalar2=30000.0, op0=ALU.add, op1=ALU.mult),
                 r=[maskt[i][1]], w=[negm[i][1]])
    S.op("dve", lambda e: e.memset(onesD[:], 1.0 / D), w=[b_onesD])
    S.op("dve", lambda e: e.tensor_copy(out=identb[:], in_=ident[:]), r=[b_ident], w=[b_identb])
    for i in range(4):
        load_cols(lnc[:, i, :], b_lnc, ln1[i, :], KC)
        load_cols(lnc[:, 4 + i, :], b_lnc, ln2[i, :], KC)

    g.cv = 0

    def convert(src, dst, K, N, pw=256):
        for kc in range(K // 128):
            for n0 in range(0, N, 512):
                w = min(512, N - n0)
                i = g.cv % 2
                g.cv += 1
                st, stb_ = cst[i]
                sbt, sbb_ = pT[i]
                S.dma("sp", st[:, 0:w], src[kc * 128:(kc + 1) * 128, n0:n0 + w], w=[stb_])
                evac(sbt[:, 0:w], sbb_, st[:, 0:w], stb_)
                for q0 in range(0, w, pw):
                    cw = min(pw, w - q0)
                    S.dma("pool", dst[(n0 + q0) // pw, :, kc, 0:cw], sbt[:, q0:q0 + cw], r=[sbb_], w=[bW])

    for l in layers:
        if ffn:
            convert(ffn_w_in[l], ffn_wi_b[l], D, 2 * FFN_D)
            convert(ffn_w_out[l], ffn_wo_b[l], FFN_D, D, pw=128)
    if 0 in MIX:
        convert(ssd_w_in[0], ssd_wi_b, D, 10368)
        convert(ssd_w_out[0], ssd_wo_b, 4096, D, pw=128)
    if 2 in MIX:
        convert(mlstm_w_in[0], mlstm_wi_b, D, 6176)
        convert(mlstm_w_out[0], mlstm_wo_b, D, D, pw=128)
    if 3 in MIX:
        convert(hgrn_w_in[0], hgrn_wi_b, D, 10240)
        convert(hgrn_w_out[0], hgrn_wo_b, D, D, pw=128)
    if 1 in MIX:
        convert(attn_w_qkv[0], attn_wqkv_b, D, 6144)
        convert(attn_w_out[0], attn_wo_b, D, D, pw=128)

    xin4 = [cst[0], cst[1]]
    for b in range(NBLK):
        t0 = b * TB
        for qt in range(4):
            pbs = [bank() for _ in range(4)]
            for j in range(4):
                st, stb_ = xin4[j % 2]
                S.dma("sp", st[:, 0:512], x_in[t0 + j * 128:t0 + (j + 1) * 128, qt * 512:(qt + 1) * 512], w=[stb_])
                for c in range(4):
                    pt, pb = pbs[c]
                    tr(pt[:, j * 128:(j + 1) * 128], pb, st[:, c * 128:(c + 1) * 128], stb_)
            for c in range(4):
                pt, pb = pbs[c]
                evac(xs[:, qt * 4 + c, 0:TB], b_xs, pt[:, :], pb)
        S.dma("pool", xT_v2[0][:, :, t0:t0 + TB], xs[:, :, 0:TB], r=[b_xs], w=[bxT2[0][b]])

    def norm_block(b, lnidx, halo):
        t0 = b * TB
        W = TB + 2 * halo
        xT_v, bxT = xT_v2[g.cur], bxT2[g.cur]
        S.dma("sp", xs[:, :, 0:TB], xT_v[:, :, t0:t0 + TB], r=[bxT[b]], w=[b_xs])
        if halo:
            for side in range(2):
                c0 = TB + side * halo
                tt = t0 - halo if side == 0 else t0 + TB
                edge = t0 if side == 0 else t0 + TB
                if edge <= 0 or edge >= TOK:
                    S.op("dve", lambda e: e.memset(xs[:, :, c0:c0 + halo], 0.0), w=[b_xs])
                else:
                    nb = b - 1 if side == 0 else b + 1
                    S.dma("sp", xs[:, :, c0:c0 + halo], xT_v[:, :, tt:tt + halo], r=[bxT[nb]], w=[b_xs], allow_slow_non_contiguous=True)
                    if edge % SL == 0:
                        S.op("dve", lambda e: e.tensor_scalar(out=xs[:, :, c0:c0 + halo], in0=xs[:, :, c0:c0 + halo],
                                                              scalar1=linkt[:, 0:1], scalar2=None, op0=ALU.mult),
                             r=[b_link], w=[b_xs])
        pm, pmb = bank()
        ph, phb = bank() if halo else (None, None)
        for c in range(KC):
            sqt, sqb = sq[c % 2]
            S.op("act", lambda e: e.activation(out=sqt[:, 0:W], in_=xs[:, c, 0:W], func=AF.Square), r=[b_xs], w=[sqb])
            mm(pm[:, :], pmb, onesD[:], b_onesD, sqt[:, 0:TB], sqb, c == 0, c == KC - 1)
            if halo:
                mm(ph[:, 0:2 * halo], phb, onesD[:], b_onesD, sqt[:, TB:W], sqb, c == 0, c == KC - 1)
        S.op("act", lambda e: e.activation(out=rstd[:, 0:TB], in_=pm[:, :], func=AF.Sqrt, bias=epst[:, 0:1], scale=1.0),
             r=[pmb, b_eps], w=[b_rstd])
        if halo:
            S.op("act", lambda e: e.activation(out=rstd[:, TB:W], in_=ph[:, 0:2 * halo], func=AF.Sqrt, bias=epst[:, 0:1], scale=1.0),
                 r=[phb, b_eps], w=[b_rstd])
        S.op("dve", lambda e: e.reciprocal(out=rstd[:, 0:W], in_=rstd[:, 0:W]), r=[b_rstd], w=[b_rstd])
        for c in range(KC):
            S.op("dve", lambda e: e.scalar_tensor_tensor(out=hT[:, c, 0:W], in0=xs[:, c, 0:W], scalar=lnc[:, lnidx, c:c + 1],
                                                         in1=rstd[:, 0:W], op0=ALU.mult, op1=ALU.mult),
                 r=[b_xs, b_lnc, b_rstd], w=[b_hT])

    g.wp_rr = 0
    g.wo_rr = 0

    def load_panel(Wb, n0, w=256):
        t, tb_ = wp[g.wp_rr % len(wp)]
        g.wp_rr += 1
        assert n0 % 256 == 0
        S.dma("sp", t[:, :, 0:w], Wb[n0 // 256][:, :, 0:w], r=[bW], w=[tb_])
        return t, tb_

    def gemm_group(pt, pb, wt, wb_, col0, tok0, ntok):
        for kc in range(KC):
            mm(pt[:, 0:ntok], pb, wt[:, kc, col0:col0 + 128], wb_, hT[:, kc, tok0:tok0 + ntok], b_hT, kc == 0, kc == KC - 1)

    def out_proj(b, Wb, KCo):
        t0 = b * TB
        xT_v, bxT = xT_v2[g.cur], bxT2[g.cur]
        xT_w, bxT_w = xT_v2[1 - g.cur], bxT2[1 - g.cur]
        for dch in range(D // 128):
            t, tb_ = wo[g.wo_rr % 2]
            g.wo_rr += 1
            S.dma("sp", t[:, 0:KCo, :], Wb[dch], r=[bW], w=[tb_])
            pt, pb = bank()
            for kc in range(KCo):
                mm(pt[:, :], pb, t[:, kc, :], tb_, actT[:, kc, :], b_actT, kc == 0, kc == KCo - 1)
            xt, xb_ = xr[dch % 2]
            S.dma("sp", xt[:, :], xT_v[:, dch, t0:t0 + TB], r=[bxT[b]], w=[xb_])
            S.op("dve", lambda e: e.tensor_tensor(out=xt[:, :], in0=xt[:, :], in1=pt[:, :], op=ALU.add), r=[pb, xb_], w=[xb_])
            S.dma("pool", xT_w[:, dch, t0:t0 + TB], xt[:, :], r=[xb_], w=[bxT_w[b]])

    def ffn_layer(l):
        for k in range(3):
            load_cols(fcw[:, k, :], b_fcw, ffn_conv_w[l, k, :], 88)
        load_cols(fcw[:, 3, :], b_fcw, ffn_conv_b[l, :], 88)
        for b in range(NBLK):
            norm_block(b, 4 + l, 1)
            for p in range(FFN_D // 256):
                wa, wab = load_panel(ffn_wi_b[l], p * 256)
                wb2, wbb = load_panel(ffn_wi_b[l], FFN_D + p * 256)
                for cc in range(2):
                    ch = p * 2 + cc
                    res = []
                    for (wt, wtb, u, ubuf, fch) in ((wa, wab, ua, b_ua, ch), (wb2, wbb, ub, b_ub, 44 + ch)):
                        pm, pmb = bank()
                        gemm_group(pm, pmb, wt, wtb, cc * 128, 0, TB)
                        ph, phb = bank()
                        gemm_group(ph, phb, wt, wtb, cc * 128, TB, 2)
                        evac(u[:, 1:TB + 1], ubuf, pm[:, :], pmb, eng="act")
                        evac(u[:, 0:1], ubuf, ph[:, 0:1], phb, eng="act")
                        evac(u[:, TB + 1:TB + 2], ubuf, ph[:, 1:2], phb, eng="act")
                        res.append(fch)
                    for (u, ubuf, acc, accbuf, fch) in ((ua, b_ua, acca, b_acca, ch), (ub, b_ub, accb, b_accb, 44 + ch)):
                        S.op("dve", lambda e: e.tensor_scalar(out=acc[:, :], in0=u[:, 1:TB + 1], scalar1=fcw[:, 1, fch:fch + 1],
                                                              scalar2=fcw[:, 3, fch:fch + 1], op0=ALU.mult, op1=ALU.add),
                             r=[ubuf, b_fcw], w=[accbuf])
                        S.op("dve", lambda e: e.scalar_tensor_tensor(out=acc[:, :], in0=u[:, 0:TB], scalar=fcw[:, 0, fch:fch + 1],
                                                                     in1=acc[:, :], op0=ALU.mult, op1=ALU.add),
                             r=[ubuf, b_fcw, accbuf], w=[accbuf])
                        S.op("dve", lambda e: e.scalar_tensor_tensor(out=acc[:, :], in0=u[:, 2:TB + 2], scalar=fcw[:, 2, fch:fch + 1],
                                                                     in1=acc[:, :], op0=ALU.mult, op1=ALU.add),
                             r=[ubuf, b_fcw, accbuf], w=[accbuf])
                    S.op("act", lambda e: e.activation(out=sa[:, :], in_=acca[:, :], func=AF.Silu), r=[b_acca], w=[b_sa])
                    S.op("dve", lambda e: e.tensor_tensor(out=actT[:, ch, :], in0=sa[:, :], in1=accb[:, :], op=ALU.mult),
                         r=[b_sa, b_accb], w=[b_actT])
            out_proj(b, ffn_wo_b[l], 44)
        g.cur = 1 - g.cur


    def mixer_out(b, Wb, KCo):
        t0 = b * TB
        S.dma("sp", actT[:, 0:KCo, :], yT_v[:, 0:KCo, t0:t0 + TB], r=[byT[b]], w=[b_actT])
        out_proj(b, Wb, KCo)


    CPS = SL // 128

    def tok_major_proj(b, Wb, col0, ncols, dst_d, dst_b, dcol0, func=None):
        t0 = b * TB
        for p0 in range(0, ncols, 256):
            wt, wtb = load_panel(Wb, col0 + p0)
            for tt in range(4):
                pv, pvb = bank()
                for kc in range(KC):
                    mm(pv[:, 0:256], pvb, hT[:, kc, tt * 128:(tt + 1) * 128], b_hT, wt[:, kc, 0:256], wtb, kc == 0, kc == KC - 1)
                ot_, otb_ = st16[tt % 2]
                if func is None:
                    evac(ot_[:, 0:256], otb_, pv[:, 0:256], pvb)
                else:
                    S.op("act", lambda e: e.activation(out=ot_[:, 0:256], in_=pv[:, 0:256], func=func), r=[pvb], w=[otb_])
                S.dma("pool", dst_d[t0 + tt * 128:t0 + (tt + 1) * 128, dcol0 + p0:dcol0 + p0 + 256], ot_[:, 0:256], r=[otb_], w=[dst_b])

    def hgrn_layer(l):
        assert l == 3
        for i in range(4):
            load_cols(pcol[:, i, :], b_pcol, hgrn_lb[i, :], 16)
        S.op("act", lambda e: e.activation(out=pcol[:, 0:4, :], in_=pcol[:, 0:4, :], func=AF.Exp), r=[b_pcol], w=[b_pcol])
        S.op("dve", lambda e: e.tensor_tensor(out=pcol[:, 4, :], in0=pcol[:, 1, :], in1=pcol[:, 2, :], op=ALU.add), r=[b_pcol], w=[b_pcol])
        S.op("dve", lambda e: e.tensor_tensor(out=pcol[:, 4, :], in0=pcol[:, 4, :], in1=pcol[:, 3, :], op=ALU.add), r=[b_pcol], w=[b_pcol])
        S.op("dve", lambda e: e.tensor_tensor(out=pcol[:, 5, :], in0=pcol[:, 4, :], in1=pcol[:, 0, :], op=ALU.add), r=[b_pcol], w=[b_pcol])
        S.op("dve", lambda e: e.reciprocal(out=pcol[:, 5, :], in_=pcol[:, 5, :]), r=[b_pcol], w=[b_pcol])
        S.op("dve", lambda e: e.tensor_tensor(out=pcol[:, 6, :], in0=pcol[:, 4, :], in1=pcol[:, 5, :], op=ALU.mult), r=[b_pcol], w=[b_pcol])
        S.op("dve", lambda e: e.tensor_tensor(out=pcol[:, 7, :], in0=pcol[:, 0, :], in1=pcol[:, 5, :], op=ALU.mult), r=[b_pcol], w=[b_pcol])
        load_cols(pcol[:, 3, :], b_pcol, hgrn_norm_w[0, :], 16)
        LB, OML, NW = 6, 7, 3
        for b in range(NBLK):
            t0 = b * TB
            norm_block(b, l, 0)
            for hd in range(16):
                pq, pqb = bank()
                wt, wtb = load_panel(hgrn_wi_b, (hd // 2) * 256)
                gemm_group(pq, pqb, wt, wtb, (hd % 2) * 128, 0, TB)
                for dr in range(2):
                    pf, pfb = bank()
                    wt, wtb = load_panel(hgrn_wi_b, 2048 + dr * 2048 + (hd // 2) * 256)
                    gemm_group(pf, pfb, wt, wtb, (hd % 2) * 128, 0, TB)
                    f_, fb_ = w32[0]
                    k_, kb_ = w32[1]
                    lg, lgb = w32[2]
                    bc, bcb = w32[3]
                    S.op("act", lambda e: e.activation(out=f_[:, :], in_=pf[:, :], func=AF.Sigmoid), r=[pfb], w=[fb_])
                    S.op("dve", lambda e: e.tensor_scalar(out=f_[:, :], in0=f_[:, :], scalar1=pcol[:, OML, hd:hd + 1], scalar2=pcol[:, LB, hd:hd + 1],
                                                          op0=ALU.mult, op1=ALU.add), r=[fb_, b_pcol], w=[fb_])
                    S.op("dve", lambda e: e.tensor_scalar(out=k_[:, :], in0=f_[:, :], scalar1=-1.0, scalar2=1.0, op0=ALU.mult, op1=ALU.add), r=[fb_], w=[kb_])
                    S.op("act", lambda e: e.activation(out=lg[:, :], in_=f_[:, :], func=AF.Ln), r=[fb_], w=[lgb])
                    for sgi in range(4):
                        sl_ = slice(sgi * 128, (sgi + 1) * 128)
                        S.op("dve", lambda e: e.tensor_tensor_scan(out=bc[:, sl_], data0=ones512[:, sl_], data1=lg[:, sl_], initial=0.0,
                                                                   op0=ALU.mult, op1=ALU.add), r=[b_ones512, lgb], w=[bcb])
                    S.op("act", lambda e: e.activation(out=ebs[:, 0:4], in_=bc[:, 127:512:128], func=AF.Exp), r=[bcb], w=[b_ebs])
                    S.dma("pool", eb_d[dr, hd, :, b * 4:(b + 1) * 4], ebs[:, 0:4], r=[b_ebs], w=[b_eb])
                    if dr == 1:
                        S.op("dve", lambda e: e.tensor_copy(out=ebs[:, 4:8], in_=bc[:, 127:512:128]), r=[bcb], w=[b_ebs])
                        S.op("dve", lambda e: e.tensor_tensor(out=f_[:, :], in0=lg[:, :], in1=bc[:, 0:TB], op=ALU.subtract), r=[lgb, bcb], w=[fb_])
                        for sgi in range(4):
                            sl_ = slice(sgi * 128, (sgi + 1) * 128)
                            S.op("dve", lambda e: e.tensor_scalar(out=bc[:, sl_], in0=f_[:, sl_], scalar1=ebs[:, 4 + sgi:5 + sgi], scalar2=None,
                                                                  op0=ALU.add), r=[fb_, b_ebs], w=[bcb])
                    S.op("act", lambda e: e.activation(out=f_[:, :], in_=bc[:, 0:TB], func=AF.Exp), r=[bcb], w=[fb_])
                    qo, qob = st16[0]
                    S.op("dve", lambda e: e.tensor_tensor(out=qo[:, :], in0=pq[:, :], in1=f_[:, :], op=ALU.mult), r=[pqb, fb_], w=[qob])
                    S.dma("pool", qd_d[dr, hd, :, t0:t0 + TB], qo[:, :], r=[qob], w=[b_qd])
                    S.op("act", lambda e: e.activation(out=lg[:, :], in_=bc[:, 0:TB], func=AF.Exp, scale=-1.0), r=[bcb], w=[lgb])
                    S.op("dve", lambda e: e.tensor_tensor(out=k_[:, :], in0=k_[:, :], in1=lg[:, :], op=ALU.mult), r=[kb_, lgb], w=[kb_])
                    ko, kob = st16[1]
                    S.op("act", lambda e: e.copy(out=ko[:, :], in_=k_[:, :]), r=[kb_], w=[kob])
                    S.dma("pool", ki_d[dr, hd, :, t0:t0 + TB], ko[:, :], r=[kob], w=[b_ki])
                    pt, pb = bank()
                    for sgi in range(4):
                        tr(pt[:, sgi * 128:(sgi + 1) * 128], pb, k_[:, sgi * 128:(sgi + 1) * 128], kb_)
                    evac(kst[:, :, :].rearrange("p a b -> p (a b)"), b_kst, pt[:, :], pb)
                    S.dma("pool", kitm_d[dr, hd, t0:t0 + TB, :].rearrange("(j s) k -> s j k", s=128), kst[:, :, :], r=[b_kst], w=[b_kitm])
                pg, pgb = bank()
                wt, wtb = load_panel(hgrn_wi_b, 8192 + (hd // 2) * 256)
                gemm_group(pg, pgb, wt, wtb, (hd % 2) * 128, 0, TB)
                go, gob = st16[hd % 2]
                S.op("act", lambda e: e.activation(out=go[:, :], in_=pg[:, :], func=AF.Silu), r=[pgb], w=[gob])
                S.dma("pool", g_d[hd, :, t0:t0 + TB], go[:, :], r=[gob], w=[b_gd])
            tok_major_proj(b, hgrn_wi_b, 6144, 2048, vr_d, b_vr, 0)

        flatA = actT[:, :, :].rearrange("p a b -> p (a b)")
        QD, QDb = hT[:, :, :].rearrange("p a b -> p (a b)")[:, 0:TOK], b_hT
        KI, KIb = kT2[:, 0:TOK], b_kT2
        KTM = flatA[:, 0:TOK].rearrange("p (j k) -> p j k", k=128)
        VV = flatA[:, TOK:2 * TOK].rearrange("p (j k) -> p j k", k=128)
        for hd in range(16):
            S.dma("sp", VV, vr_d.rearrange("(j s) e -> s j e", s=128)[:, :, hd * 128:(hd + 1) * 128], r=[b_vr], w=[b_actT])
            for dr in range(2):
                S.dma("sp", QD, qd_d[dr, hd], r=[b_qd], w=[QDb])
                S.dma("sp", KI, ki_d[dr, hd], r=[b_ki], w=[KIb])
                S.dma("sp", KTM, kitm_d[dr, hd].rearrange("(j s) k -> s j k", s=128), r=[b_kitm], w=[b_actT])
                S.dma("sp", ebt[:, :], eb_d[dr, hd], r=[b_eb], w=[b_ebt])
                S.op("dve", lambda e: e.memset(S32[:, 0:128], 0.0), w=[b_S32])
                S.op("dve", lambda e: e.memset(Sb[:, 0:128], 0.0), w=[b_Sb])
                order = range(NT) if dr == 0 else range(NT - 1, -1, -1)
                ost, ostb = w32[dr]
                for c in order:
                    cs = slice(c * 128, (c + 1) * 128)
                    boundary = (c % CPS == 0 and c > 0) if dr == 0 else ((c + 1) % CPS == 0 and c < NT - 1)
                    if boundary:
                        S.op("dve", lambda e: e.tensor_scalar(out=S32[:, 0:128], in0=S32[:, 0:128], scalar1=linkt[:, 0:1], scalar2=None, op0=ALU.mult),
                             r=[b_link, b_S32], w=[b_S32])
                        S.op("act", lambda e: e.copy(out=Sb[:, 0:128], in_=S32[:, 0:128]), r=[b_S32], w=[b_Sb])
                    ps_, psb_ = bank()
                    mm(ps_[:, 0:128], psb_, KI[:, cs], KIb, QD[:, cs], QDb, True, True)
                    a_, ab_ = at_[c % 2]
                    S.op("dve", lambda e: e.tensor_tensor(out=a_[:, :], in0=ps_[:, 0:128], in1=maskt[dr][0][:, :], op=ALU.mult), r=[psb_, maskt[dr][1]], w=[ab_])
                    po, pob = bank()
                    mm(po[:, 0:128], pob, VV[:, c, :], b_actT, a_[:, :], ab_, True, False)
                    mm(po[:, 0:128], pob, Sb[:, 0:128], b_Sb, QD[:, cs], QDb, False, True)
                    cc = c % 4
                    evac(ost[:, cc * 128:(cc + 1) * 128], ostb, po[:, 0:128], pob)
                    pd, pdb = bank()
                    mm(pd[:, 0:128], pdb, KTM[:, c, :], b_actT, VV[:, c, :], b_actT, True, True)
                    S.op("dve", lambda e: e.tensor_scalar(out=S32[:, 0:128], in0=S32[:, 0:128], scalar1=ebt[:, c:c + 1], scalar2=None, op0=ALU.mult),
                         r=[b_ebt, b_S32], w=[b_S32])
                    S.op("dve", lambda e: e.scalar_tensor_tensor(out=S32[:, 0:128], in0=pd[:, 0:128], scalar=ebt[:, c:c + 1], in1=S32[:, 0:128],
                                                                 op0=ALU.mult, op1=ALU.add), r=[pdb, b_ebt, b_S32], w=[b_S32])
                    S.op("act", lambda e: e.copy(out=Sb[:, 0:128], in_=S32[:, 0:128]), r=[b_S32], w=[b_Sb])
                    blk_done = (cc == 3) if dr == 0 else (cc == 0)
                    if blk_done:
                        bq = c // 4
                        t0 = bq * TB
                        if dr == 0:
                            S.dma("pool", oT_d[hd, :, t0:t0 + TB], ost[:, :], r=[ostb], w=[b_oT])
                        else:
                            of, ofb = w32[2]
                            S.dma("sp", of[:, :], oT_d[hd, :, t0:t0 + TB], r=[b_oT], w=[ofb])
                            S.op("dve", lambda e: e.tensor_tensor(out=of[:, :], in0=of[:, :], in1=ost[:, :], op=ALU.add), r=[ofb, ostb], w=[ofb])
                            sqt, sqb = sq[0]
                            S.op("act", lambda e: e.activation(out=sqt[:, 0:TB], in_=of[:, :], func=AF.Square), r=[ofb], w=[sqb])
                            pn, pnb = bank()
                            mm(pn[:, :], pnb, ones128[:], b_ones128, sqt[:, 0:TB], sqb, True, True)
                            S.op("act", lambda e: e.activation(out=rstd[:, 0:TB], in_=pn[:, :], func=AF.Sqrt, bias=epst[:, 0:1], scale=1.0),
                                 r=[pnb, b_eps], w=[b_rstd])
                            S.op("dve", lambda e: e.reciprocal(out=rstd[:, 0:TB], in_=rstd[:, 0:TB]), r=[b_rstd], w=[b_rstd])
                            S.op("dve", lambda e: e.scalar_tensor_tensor(out=of[:, :], in0=of[:, :], scalar=pcol[:, NW, hd:hd + 1], in1=rstd[:, 0:TB],
                                                                         op0=ALU.mult, op1=ALU.mult), r=[ofb, b_pcol, b_rstd], w=[ofb])
                            gt, gtb = st16[0]
                            S.dma("sp", gt[:, :], g_d[hd, :, t0:t0 + TB], r=[b_gd], w=[gtb])
                            yo_, yob_ = st16[1]
                            S.op("dve", lambda e: e.tensor_tensor(out=yo_[:, :], in0=of[:, :], in1=gt[:, :], op=ALU.mult), r=[ofb, gtb], w=[yob_])
                            S.dma("pool", yT_v[:, hd, t0:t0 + TB], yo_[:, :], r=[yob_], w=[byT[bq]])
        for b in range(NBLK):
            mixer_out(b, hgrn_wo_b, 16)
        g.cur = 1 - g.cur


    def decay_tables(NH, a_col, i_col):
        W = 2 * NH
        TABv = xsflat[:, 0:NT * 5 * W].rearrange("p (c k j) -> p c k j", k=5, j=W)
        gl, glb = w32[0]
        for c in range(NT):
            S.dma("sp", gl[:, 0:128], gates_d[c * 128:(c + 1) * 128, :], r=[b_gates], w=[glb])
            for dr in range(2):
                js = slice(dr * NH, (dr + 1) * NH)
                acol = gl[:, a_col(dr):a_col(dr) + NH]
                pa, pab = bank()
                mm(pa[:, 0:NH], pab, maskt[dr][0][:, :], maskt[dr][1], acol, glb, True, True)
                ptt, ptb = bank()
                mm(ptt[:, 0:NH], ptb, ones[:], b_ones, acol, glb, True, True)
                S.op("dve", lambda e: e.tensor_copy(out=TABv[:, c, 0, js], in_=pa[:, 0:NH]), r=[pab], w=[b_xs])
                if i_col is not None:
                    S.op("dve", lambda e: e.tensor_tensor(out=TABv[:, c, 1, js], in0=gl[:, i_col(dr):i_col(dr) + NH], in1=pa[:, 0:NH], op=ALU.subtract),
                         r=[glb, pab], w=[b_xs])
                else:
                    S.op("dve", lambda e: e.tensor_scalar(out=TABv[:, c, 1, js], in0=pa[:, 0:NH], scalar1=-1.0, scalar2=None, op0=ALU.mult), r=[pab], w=[b_xs])
                S.op("act", lambda e: e.activation(out=TABv[:, c, 2, js], in_=pa[:, 0:NH], func=AF.Exp), r=[pab], w=[b_xs])
                S.op("dve", lambda e: e.tensor_tensor(out=TABv[:, c, 3, js], in0=TABv[:, c, 1, js], in1=ptt[:, 0:NH], op=ALU.add), r=[ptb, b_xs], w=[b_xs])
                S.op("act", lambda e: e.activation(out=TABv[:, c, 3, js], in_=TABv[:, c, 3, js], func=AF.Exp), r=[b_xs], w=[b_xs])
                S.op("act", lambda e: e.activation(out=TABv[:, c, 4, js], in_=ptt[:, 0:NH], func=AF.Exp), r=[ptb], w=[b_xs])
        return TABv

    def decay_matrix(TABv, c, dr, j, acol_ap, acol_b):
        S.op("dve", lambda e: e.tensor_scalar(out=abc[:, :], in0=ones[:, :], scalar1=acol_ap, scalar2=None, op0=ALU.mult), r=[b_ones, acol_b], w=[b_abc])
        pe_, peb = bank()
        mm(pe_[:, 0:128], peb, abc[:, :], b_abc, maskt[dr][0][:, :], maskt[dr][1], True, False)
        mm(pe_[:, 0:128], peb, ident[:, :], b_ident, negm[dr][0][:, :], negm[dr][1], False, True)
        e_, eb_ = Et[g.et_rr % 2]
        g.et_rr += 1
        S.op("act", lambda e: e.activation(out=e_[:, :], in_=pe_[:, 0:128], func=AF.Exp, bias=TABv[:, c, 1, j:j + 1], scale=1.0), r=[peb, b_xs], w=[eb_])
        return e_, eb_

    g.et_rr = 0

    def mlstm_layer(l):
        NH = 8
        bcast_row(gbc[:, 0:32], b_gbc, mlstm_gate_b[0].rearrange("a h -> (a h)").rearrange("(o n) -> o n", o=1), 32)
        for b in range(NBLK):
            t0 = b * TB
            norm_block(b, l, 0)
            for h in range(NH):
                for which in range(2):
                    pq, pqb = bank()
                    wt, wtb = load_panel(mlstm_wi_b, which * 1024 + (h // 2) * 256)
                    gemm_group(pq, pqb, wt, wtb, (h % 2) * 128, 0, TB)
                    qo, qob = st16[which]
                    if which == 0:
                        S.op("dve", lambda e: e.tensor_scalar(out=qo[:, :], in0=pq[:, :], scalar1=128 ** -0.5, scalar2=None, op0=ALU.mult), r=[pqb], w=[qob])
                        S.dma("pool", qd_d[0, h, :, t0:t0 + TB], qo[:, :], r=[qob], w=[b_qd])
                    else:
                        evac(qo[:, :], qob, pq[:, :], pqb)
                        S.dma("pool", ki_d[0, h, :, t0:t0 + TB], qo[:, :], r=[qob], w=[b_ki])
            tok_major_proj(b, mlstm_wi_b, 1024, 1024, k2_d, b_k2, 0)
            tok_major_proj(b, mlstm_wi_b, 2048, 2048, vr_d, b_vr, 0)
            tok_major_proj(b, mlstm_wi_b, 4096, 2048, og_d, b_og, 0, func=AF.Sigmoid)
            wt, wtb = load_panel(mlstm_wi_b, 6144, w=32)
            for tt in range(4):
                pg, pgb = bank()
                for kc in range(KC):
                    mm(pg[:, 0:32], pgb, hT[:, kc, tt * 128:(tt + 1) * 128], b_hT, wt[:, kc, 0:32], wtb, kc == 0, kc == KC - 1)
                gt, gtb = w32[tt % 2]
                S.op("dve", lambda e: e.tensor_tensor(out=gt[:, 0:32], in0=pg[:, 0:32], in1=gbc[:, 0:32], op=ALU.add), r=[pgb, b_gbc], w=[gtb])
                fv = gt[:, 0:32].rearrange("p (a b) -> p a b", b=8)[:, 1:4:2, :]
                S.op("act", lambda e: e.activation(out=fv, in_=fv, func=AF.Exp, scale=-1.0), r=[gtb], w=[gtb])
                S.op("dve", lambda e: e.tensor_scalar(out=fv, in0=fv, scalar1=1.0, scalar2=None, op0=ALU.add), r=[gtb], w=[gtb])
                S.op("act", lambda e: e.activation(out=fv, in_=fv, func=AF.Ln), r=[gtb], w=[gtb])
                S.op("dve", lambda e: e.tensor_scalar(out=fv, in0=fv, scalar1=-1.0, scalar2=None, op0=ALU.mult), r=[gtb], w=[gtb])
                S.dma("pool", gates_d[t0 + tt * 128:t0 + (tt + 1) * 128, 0:32], gt[:, 0:32], r=[gtb], w=[b_gates])
        TABv = decay_tables(NH, lambda dr: dr * 16 + 8, lambda dr: dr * 16)
        flatA = actT[:, :, :].rearrange("p a b -> p (a b)")
        QT, QTb = hT[:, :, :].rearrange("p a b -> p (a b)")[:, 0:TOK], b_hT
        KT, KTb = kT2[:, 0:TOK], b_kT2
        V1 = flatA[:, 0:NT * 258].rearrange("p (j e) -> p j e", e=258)
        gl, glb = w32[1]
        for h in range(NH):
            S.dma("sp", V1[:, :, 0:256], vr_d.rearrange("(j s) e -> s j e", s=128)[:, :, h * 256:(h + 1) * 256], r=[b_vr], w=[b_actT])
            S.op("dve", lambda e: e.memset(V1[:, :, 256:257], 1.0), w=[b_actT])
            S.dma("sp", QT, qd_d[0, h], r=[b_qd], w=[QTb])
            S.dma("sp", KT, ki_d[0, h], r=[b_ki], w=[KTb])
            bcast_row(nwbc[:, 0:256], b_nwbc, mlstm_norm_w[0:1, h * 256:(h + 1) * 256], 256)
            for dr in range(2):
                j = dr * NH + h
                S.op("dve", lambda e: e.memset(S32[:, 0:258], 0.0), w=[b_S32])
                S.op("dve", lambda e: e.memset(Sb[:, 0:258], 0.0), w=[b_Sb])
                order = range(NT) if dr == 0 else range(NT - 1, -1, -1)
                for c in order:
                    cs = slice(c * 128, (c + 1) * 128)
                    boundary = (c % CPS == 0 and c > 0) if dr == 0 else ((c + 1) % CPS == 0 and c < NT - 1)
                    if boundary:
                        S.op("dve", lambda e: e.tensor_scalar(out=S32[:, 0:258], in0=S32[:, 0:258], scalar1=linkt[:, 0:1], scalar2=None, op0=ALU.mult),
                             r=[b_link, b_S32], w=[b_S32])
                        S.op("act", lambda e: e.copy(out=Sb[:, 0:258], in_=S32[:, 0:258]), r=[b_S32], w=[b_Sb])
                    S.dma("sp", gl[:, 0:32], gates_d[c * 128:(c + 1) * 128, 0:32], r=[b_gates], w=[glb])
                    kt_, ktb_ = ktc[c % 2]
                    S.dma("sp", kt_[:, :], k2_d[c * 128:(c + 1) * 128, h * 128:(h + 1) * 128], r=[b_k2], w=[ktb_])
                    e_, eb_ = decay_matrix(TABv, c, dr, j, gl[:, dr * 16 + 8 + h:dr * 16 + 9 + h], glb)
                    ps_, psb_ = bank()
                    mm(ps_[:, 0:128], psb_, KT[:, cs], KTb, QT[:, cs], QTb, True, True)
                    a_, ab_ = at_[c % 2]
                    S.op("dve", lambda e: e.tensor_tensor(out=a_[:, :], in0=ps_[:, 0:128], in1=e_[:, :], op=ALU.mult), r=[psb_, eb_], w=[ab_])
                    pn, pnb = bank()
                    mm(pn[:, 0:257], pnb, a_[:, :], ab_, V1[:, c, 0:257], b_actT, True, True)
                    pi_, pib = bank()
                    mm(pi_[:, 0:257], pib, QT[:, cs], QTb, Sb[:, 0:257], b_Sb, True, True)
                    S.op("act", lambda e: e.copy(out=o32[:, 0:257], in_=pn[:, 0:257]), r=[pnb], w=[b_o32])
                    S.op("dve", lambda e: e.scalar_tensor_tensor(out=o32[:, 0:257], in0=pi_[:, 0:257], scalar=TABv[:, c, 2, j:j + 1], in1=o32[:, 0:257],
                                                                 op0=ALU.mult, op1=ALU.add), r=[pib, b_xs, b_o32], w=[b_o32])
                    S.op("dve", lambda e: e.tensor_scalar(out=kwt[:, :], in0=kt_[:, :], scalar1=TABv[:, c, 3, j:j + 1], scalar2=None, op0=ALU.mult),
                         r=[ktb_, b_xs], w=[b_kwt])
                    pd, pdb = bank()
                    mm(pd[:, 0:257], pdb, kwt[:, :], b_kwt, V1[:, c, 0:257], b_actT, True, True)
                    S.op("dve", lambda e: e.scalar_tensor_tensor(out=S32[:, 0:257], in0=S32[:, 0:257], scalar=TABv[:, c, 4, j:j + 1], in1=pd[:, 0:257],
                                                                 op0=ALU.mult, op1=ALU.add), r=[pdb, b_xs, b_S32], w=[b_S32])
                    S.op("act", lambda e: e.copy(out=Sb[:, 0:258], in_=S32[:, 0:258]), r=[b_S32], w=[b_Sb])
                    S.op("dve", lambda e: e.tensor_scalar(out=o32[:, 257:258], in0=o32[:, 256:257], scalar1=-1.0, scalar2=None, op0=ALU.mult), r=[b_o32], w=[b_o32])
                    S.op("dve", lambda e: e.tensor_tensor(out=o32[:, 257:258], in0=o32[:, 257:258], in1=o32[:, 256:257], op=ALU.max), r=[b_o32], w=[b_o32])
                    S.op("dve", lambda e: e.tensor_scalar(out=o32[:, 257:258], in0=o32[:, 257:258], scalar1=1.0, scalar2=None, op0=ALU.max), r=[b_o32], w=[b_o32])
                    S.op("dve", lambda e: e.reciprocal(out=o32[:, 257:258], in_=o32[:, 257:258]), r=[b_o32], w=[b_o32])
                    S.op("dve", lambda e: e.tensor_scalar(out=o32[:, 0:256], in0=o32[:, 0:256], scalar1=o32[:, 257:258], scalar2=None, op0=ALU.mult),
                         r=[b_o32], w=[b_o32])
                    if dr == 0:
                        S.dma("pool", hfw_d[c * 128:(c + 1) * 128, h * 256:(h + 1) * 256], o32[:, 0:256], r=[b_o32], w=[b_hfw])
                    else:
                        hf, hfb = w32[2]
                        S.dma("sp", hf[:, 0:256], hfw_d[c * 128:(c + 1) * 128, h * 256:(h + 1) * 256], r=[b_hfw], w=[hfb])
                        S.op("dve", lambda e: e.tensor_tensor(out=hf[:, 0:256], in0=hf[:, 0:256], in1=o32[:, 0:256], op=ALU.add), r=[hfb, b_o32], w=[hfb])
                        token_tail(hf, hfb, 256, nwbc[:, 0:256], b_nwbc, og_d, b_og, c, h * 256, 2 * h)
        for b in range(NBLK):
            mixer_out(b, mlstm_wo_b, 16)
        g.cur = 1 - g.cur

    def token_tail(hf, hfb, n, nw_ap, nw_b, gate_d, gate_b, c, gcol, ych):
        sqt, sqb = sq[0]
        S.op("act", lambda e: e.activation(out=sqt[:, 0:n], in_=hf[:, 0:n], func=AF.Square), r=[hfb], w=[sqb])
        S.op("dve", lambda e: e.reduce_sum(out=osm[:, 8:9], in_=sqt[:, 0:n], axis=AX.X), r=[sqb], w=[b_osm])
        S.op("act", lambda e: e.activation(out=osm[:, 9:10], in_=osm[:, 8:9], func=AF.Sqrt, bias=epst[:, 0:1], scale=1.0 / n), r=[b_osm, b_eps], w=[b_osm])
        S.op("dve", lambda e: e.reciprocal(out=osm[:, 10:11], in_=osm[:, 9:10]), r=[b_osm], w=[b_osm])
        S.op("dve", lambda e: e.scalar_tensor_tensor(out=hf[:, 0:n], in0=hf[:, 0:n], scalar=osm[:, 10:11], in1=nw_ap, op0=ALU.mult, op1=ALU.mult),
             r=[hfb, b_osm, nw_b], w=[hfb])
        gt, gtb = st16[0]
        S.dma("sp", gt[:, 0:n], gate_d[c * 128:(c + 1) * 128, gcol:gcol + n], r=[gate_b], w=[gtb])
        S.op("dve", lambda e: e.tensor_tensor(out=hf[:, 0:n], in0=hf[:, 0:n], in1=gt[:, 0:n], op=ALU.mult), r=[hfb, gtb], w=[hfb])
        pt, pb = bank()
        nch = n // 128
        for e2 in range(nch):
            tr(pt[:, e2 * 128:(e2 + 1) * 128], pb, hf[:, e2 * 128:(e2 + 1) * 128], hfb)
        yo_, yob_ = st16[1]
        evac(yo_[:, 0:n], yob_, pt[:, 0:n], pb)
        S.dma("pool", yT_v[:, ych:ych + nch, c * 128:(c + 1) * 128], yo_[:, 0:n].rearrange("p (a b) -> p a b", b=128), r=[yob_], w=[byT[c // 4]])


    def ssd_layer(l):
        for k in range(7):
            load_cols(ccw[:, k, :], b_ccw, ssd_conv_w[0, k, :], 48)
        load_cols(ccw[:, 7, :], b_ccw, ssd_conv_b[0, :], 48)
        load_cols(nwc[:, :], b_nwc, ssd_norm_w[0, :], 32)
        bcast_row(dtb[:, :], b_dtb, ssd_dt_bias[0].rearrange("a h -> (a h)").rearrange("(o n) -> o n", o=1), 128)
        bcast_row(nga[:, :], b_nga, ssd_a_log[0].rearrange("a h -> (a h)").rearrange("(o n) -> o n", o=1), 128)
        S.op("act", lambda e: e.activation(out=nga[:, :], in_=nga[:, :], func=AF.Exp), r=[b_nga], w=[b_nga])
        S.op("dve", lambda e: e.tensor_scalar(out=nga[:, :], in0=nga[:, :], scalar1=-1.0, scalar2=None, op0=ALU.mult), r=[b_nga], w=[b_nga])
        bcast_row(Dbc[:, :], b_Dbc, ssd_dsk[0:1, :], 64)
        for b in range(NBLK):
            t0 = b * TB
            norm_block(b, l, 3)
            for f in range(48):
                if f % 2 == 0:
                    wt, wtb = load_panel(ssd_wi_b, 4096 + (f // 2) * 256)
                pm, pmb = bank()
                gemm_group(pm, pmb, wt, wtb, (f % 2) * 128, 0, TB)
                ph, phb = bank()
                gemm_group(ph, phb, wt, wtb, (f % 2) * 128, TB, 6)
                evac(u7[:, 3:TB + 3], b_u7, pm[:, :], pmb, eng="act")
                evac(u7[:, 0:3], b_u7, ph[:, 0:3], phb, eng="act")
                evac(u7[:, TB + 3:TB + 6], b_u7, ph[:, 3:6], phb, eng="act")
                acc, accbuf = w32[f % 2]
                S.op("dve", lambda e: e.tensor_scalar(out=acc[:, :], in0=u7[:, 0:TB], scalar1=ccw[:, 0, f:f + 1], scalar2=ccw[:, 7, f:f + 1],
                                                      op0=ALU.mult, op1=ALU.add), r=[b_u7, b_ccw], w=[accbuf])
                for k in range(1, 7):
                    S.op("dve", lambda e: e.scalar_tensor_tensor(out=acc[:, :], in0=u7[:, k:k + TB], scalar=ccw[:, k, f:f + 1], in1=acc[:, :],
                                                                 op0=ALU.mult, op1=ALU.add), r=[b_u7, b_ccw, accbuf], w=[accbuf])
                S.op("act", lambda e: e.activation(out=acc[:, :], in_=acc[:, :], func=AF.Silu), r=[accbuf], w=[accbuf])
                if f >= 32:
                    fo, fob = st16[f % 2]
                    evac(fo[:, :], fob, acc[:, :], accbuf)
                    if f < 40:
                        S.dma("pool", bT_d[f - 32, :, t0:t0 + TB], fo[:, :], r=[fob], w=[b_bT])
                    else:
                        S.dma("pool", cT_d[f - 40, :, t0:t0 + TB], fo[:, :], r=[fob], w=[b_cT])
                if f < 40:
                    pt, pb = bank()
                    for sgi in range(4):
                        tr(pt[:, sgi * 128:(sgi + 1) * 128], pb, acc[:, sgi * 128:(sgi + 1) * 128], accbuf)
                    evac(kst[:, :, :].rearrange("p a b -> p (a b)"), b_kst, pt[:, :], pb)
                    if f < 32:
                        S.dma("pool", xs_d[t0:t0 + TB, f * 128:(f + 1) * 128].rearrange("(j s) k -> s j k", s=128), kst[:, :, :], r=[b_kst], w=[b_xsd])
                    else:
                        S.dma("pool", btm_d[t0:t0 + TB, (f - 32) * 128:(f - 31) * 128].rearrange("(j s) k -> s j k", s=128), kst[:, :, :], r=[b_kst], w=[b_btm])
            tok_major_proj(b, ssd_wi_b, 0, 4096, og_d, b_og, 0, func=AF.Silu)
            wt, wtb = load_panel(ssd_wi_b, 10240, w=128)
            for tt in range(4):
                pg, pgb = bank()
                for kc in range(KC):
                    mm(pg[:, 0:128], pgb, hT[:, kc, tt * 128:(tt + 1) * 128], b_hT, wt[:, kc, 0:128], wtb, kc == 0, kc == KC - 1)
                gt, gtb = w32[2 + tt % 2]
                S.op("dve", lambda e: e.tensor_tensor(out=gt[:, 0:128], in0=pg[:, 0:128], in1=dtb[:, :], op=ALU.add), r=[pgb, b_dtb], w=[gtb])
                S.op("act", lambda e: e.activation(out=gt[:, 0:128], in_=gt[:, 0:128], func=AF.Exp), r=[gtb], w=[gtb])
                S.op("dve", lambda e: e.tensor_scalar(out=gt[:, 0:128], in0=gt[:, 0:128], scalar1=1.0, scalar2=None, op0=ALU.add), r=[gtb], w=[gtb])
                S.op("act", lambda e: e.activation(out=gt[:, 0:128], in_=gt[:, 0:128], func=AF.Ln), r=[gtb], w=[gtb])
                S.dma("pool", dt_d[t0 + tt * 128:t0 + (tt + 1) * 128, :], gt[:, 0:128], r=[gtb], w=[b_dt])
                S.op("dve", lambda e: e.tensor_tensor(out=gt[:, 128:256], in0=gt[:, 0:128], in1=nga[:, :], op=ALU.mult), r=[gtb, b_nga], w=[gtb])
                S.dma("pool", gates_d[t0 + tt * 128:t0 + (tt + 1) * 128, :], gt[:, 128:256], r=[gtb], w=[b_gates])

        flatA = actT[:, :, :].rearrange("p a b -> p (a b)")
        S32s = xsflat[:, 0:4096]
        ych = xsflat[:, 4096:8192]
        Sbs = kT2[:, 0:4096]
        Vd = flatA[:, 0:4096]
        Vw = flatA[:, 4096:8192]
        Xtm = flatA[:, 8192:12288]
        Btm = flatA[:, 12288:13312]
        Zt = flatA[:, 13312:17408]
        CT = flatA[:, 17408:18432].rearrange("p (g t) -> p g t", t=128)
        BT = flatA[:, 18432:19456].rearrange("p (g t) -> p g t", t=128)
        Yt = flatA[:, 13312:17408].rearrange("p (c t) -> p c t", t=128)
        bS, bY, bSb, bVd, bVw, bX, bBt, bZ, bCT, bBT = (Buf(n) for n in ("S32s", "ych", "Sbs", "Vd", "Vw", "Xtm", "Btm", "Zt", "CT", "BT"))
        dtc, dtcb = w32[0]
        ac, acb = w32[1]
        tmp, tmpb = w32[2]
        S.op("dve", lambda e: e.memset(S32s, 0.0), r=[], w=[b_xs, bS])
        S.op("dve", lambda e: e.memset(Sbs, 0.0), w=[b_kT2, bSb])
        S.op("dve", lambda e: e.memset(Vd[:, 0:8], 0.0), w=[b_actT, bVd, bVw, bX, bBt, bZ, bCT, bBT])
        S.op("dve", lambda e: e.memset(ych[:, 0:8], 0.0), w=[b_xs, bY])
        def bc64(ap):
            n = ap.shape[1]
            return ap.unsqueeze(2).to_broadcast([128, n, 64])
        def v3(ap):
            return ap.rearrange("p (h q) -> p h q", q=64)
        for dr in range(2):
            if dr == 1:
                S.op("dve", lambda e: e.memset(S32s, 0.0), w=[bS])
                S.op("dve", lambda e: e.memset(Sbs, 0.0), w=[bSb])
            order = range(NT) if dr == 0 else range(NT - 1, -1, -1)
            hs = slice(dr * 64, (dr + 1) * 64)
            for c in order:
                rows = slice(c * 128, (c + 1) * 128)
                S.dma("sp", CT, cT_d.rearrange("g n t -> n g t")[:, :, rows], r=[b_cT], w=[bCT])
                S.dma("sp", BT, bT_d.rearrange("g n t -> n g t")[:, :, rows], r=[b_bT], w=[bBT])
                S.dma("sp", Btm, btm_d[rows, :], r=[b_btm], w=[bBt])
                S.dma("sp", Xtm, xs_d[rows, :], r=[b_xsd], w=[bX])
                S.dma("sp", dtc[:, 0:128], dt_d[rows, :], r=[b_dt], w=[dtcb])
                S.dma("sp", ac[:, 0:128], gates_d[rows, :], r=[b_gates], w=[acb])
                if dr == 1:
                    S.dma("sp", ych, hfw_d[rows, :], r=[b_hfw], w=[bY])
                pa, pab = bank()
                mm(pa[:, 0:64], pab, maskt[dr][0][:, :], maskt[dr][1], ac[:, hs], acb, True, True)
                ptt, ptb = bank()
                mm(ptt[:, 0:64], ptb, ones[:], b_ones, ac[:, hs], acb, True, True)
                S.op("dve", lambda e: e.tensor_scalar(out=TABc[:, 1, :], in0=pa[:, 0:64], scalar1=-1.0, scalar2=None, op0=ALU.mult), r=[pab], w=[b_TABc])
                S.op("act", lambda e: e.activation(out=TABc[:, 2, :], in_=pa[:, 0:64], func=AF.Exp), r=[pab], w=[b_TABc])
                S.op("dve", lambda e: e.tensor_tensor(out=TABc[:, 3, :], in0=ptt[:, 0:64], in1=TABc[:, 1, :], op=ALU.add), r=[ptb, b_TABc], w=[b_TABc])
                S.op("act", lambda e: e.activation(out=TABc[:, 3, :], in_=TABc[:, 3, :], func=AF.Exp), r=[b_TABc], w=[b_TABc])
                S.op("act", lambda e: e.activation(out=TABc[:, 4, :], in_=ptt[:, 0:64], func=AF.Exp), r=[ptb], w=[b_TABc])
                S.op("dve", lambda e: e.tensor_tensor(out=v3(Vd), in0=v3(Xtm), in1=bc64(dtc[:, hs]), op=ALU.mult), r=[bX, dtcb], w=[bVd])
                S.op("dve", lambda e: e.tensor_tensor(out=v3(Vw), in0=v3(Vd), in1=bc64(TABc[:, 3, :]), op=ALU.mult), r=[bVd, b_TABc], w=[bVw])
                boundary = (c % CPS == 0 and c > 0) if dr == 0 else ((c + 1) % CPS == 0 and c < NT - 1)
                if boundary:
                    S.op("dve", lambda e: e.tensor_scalar(out=S32s, in0=S32s, scalar1=linkt[:, 0:1], scalar2=None, op0=ALU.mult), r=[b_link, bS], w=[bS])
                    S.op("act", lambda e: e.copy(out=Sbs, in_=S32s), r=[bS], w=[bSb])
                for gq in range(8):
                    gs = slice(gq * 512, (gq + 1) * 512)
                    pS, pSb = banks[0]
                    pn, pnb = banks[1]
                    pI, pIb = banks[2]
                    pD, pDb = banks[3]
                    mm(pS[:, 0:128], pSb, BT[:, gq, :], bBT, CT[:, gq, :], bCT, True, True)
                    g.bank_pool = [4, 5, 6, 7]
                    for hh in range(8):
                        h = gq * 8 + hh
                        S.op("dve", lambda e: e.tensor_scalar(out=abc[:, :], in0=ones[:, :], scalar1=ac[:, dr * 64 + h:dr * 64 + h + 1], scalar2=None, op0=ALU.mult),
                             r=[b_ones, acb], w=[b_abc])
                        pe_, peb = bank()
                        mm(pe_[:, 0:128], peb, abc[:, :], b_abc, maskt[dr][0][:, :], maskt[dr][1], True, False)
                        mm(pe_[:, 0:128], peb, ident[:, :], b_ident, negm[dr][0][:, :], negm[dr][1], False, True)
                        e_, eb_ = Et[hh % 2]
                        S.op("act", lambda e: e.activation(out=e_[:, :], in_=pe_[:, 0:128], func=AF.Exp, bias=TABc[:, 1, h:h + 1], scale=1.0), r=[peb, b_TABc], w=[eb_])
                        a_, ab_ = at_[hh % 2]
                        S.op("dve", lambda e: e.tensor_tensor(out=a_[:, :], in0=pS[:, 0:128], in1=e_[:, :], op=ALU.mult), r=[pSb, eb_], w=[ab_])
                        mm(pn[:, hh * 64:(hh + 1) * 64], pnb, a_[:, :], ab_, Vd[:, h * 64:(h + 1) * 64], bVd, True, True)
                    g.bank_pool = list(range(8))
                    mm(pI[:, :], pIb, CT[:, gq, :], bCT, Sbs[:, gs], bSb, True, True)
                    S.op("dve", lambda e: e.tensor_tensor(out=v3(tmp[:, :]), in0=v3(pI[:, :]), in1=bc64(TABc[:, 2, gq * 8:(gq + 1) * 8]), op=ALU.mult),
                         r=[pIb, b_TABc], w=[tmpb])
                    if dr == 0:
                        S.op("dve", lambda e: e.tensor_tensor(out=ych[:, gs], in0=tmp[:, :], in1=pn[:, :], op=ALU.add), r=[tmpb, pnb], w=[bY])
                    else:
                        S.op("dve", lambda e: e.tensor_tensor(out=tmp[:, :], in0=tmp[:, :], in1=pn[:, :], op=ALU.add), r=[tmpb, pnb], w=[tmpb])
                        S.op("dve", lambda e: e.tensor_tensor(out=ych[:, gs], in0=ych[:, gs], in1=tmp[:, :], op=ALU.add), r=[tmpb, bY], w=[bY])
                    mm(pD[:, :], pDb, Btm[:, gq * 128:(gq + 1) * 128], bBt, Vw[:, gs], bVw, True, True)
                    S.op("dve", lambda e: e.tensor_tensor(out=v3(S32s[:, gs]), in0=v3(S32s[:, gs]), in1=bc64(TABc[:, 4, gq * 8:(gq + 1) * 8]), op=ALU.mult),
                         r=[bS, b_TABc], w=[bS])
                    S.op("dve", lambda e: e.tensor_tensor(out=S32s[:, gs], in0=S32s[:, gs], in1=pD[:, :], op=ALU.add), r=[bS, pDb], w=[bS])
                    S.op("act", lambda e: e.copy(out=Sbs[:, gs], in_=S32s[:, gs]), r=[bS], w=[bSb])
                if dr == 0:
                    S.dma("pool", hfw_d[rows, :], ych, r=[bY], w=[b_hfw])
                else:
                    S.dma("sp", Zt, og_d[rows, :], r=[b_og], w=[bZ])
                    for gq in range(8):
                        gs = slice(gq * 512, (gq + 1) * 512)
                        S.op("dve", lambda e: e.tensor_tensor(out=v3(tmp[:, :]), in0=v3(Xtm[:, gs]), in1=bc64(Dbc[:, gq * 8:(gq + 1) * 8]), op=ALU.mult),
                             r=[bX, b_Dbc], w=[tmpb])
                        S.op("dve", lambda e: e.tensor_tensor(out=ych[:, gs], in0=ych[:, gs], in1=tmp[:, :], op=ALU.add), r=[tmpb, bY], w=[bY])
                    S.op("dve", lambda e: e.tensor_tensor(out=ych, in0=ych, in1=Zt, op=ALU.mult), r=[bY, bZ], w=[bY])
                    for gq in range(8):
                        gs = slice(gq * 512, (gq + 1) * 512)
                        S.op("act", lambda e: e.activation(out=tmp[:, :], in_=ych[:, gs], func=AF.Square), r=[bY], w=[tmpb])
                        S.op("dve", lambda e: e.reduce_sum(out=osm[:, 16 + gq:17 + gq], in_=tmp[:, :], axis=AX.X), r=[tmpb], w=[b_osm])
                    S.op("dve", lambda e: e.reduce_sum(out=osm[:, 8:9], in_=osm[:, 16:24], axis=AX.X), r=[b_osm], w=[b_osm])
                    S.op("act", lambda e: e.activation(out=osm[:, 9:10], in_=osm[:, 8:9], func=AF.Sqrt, bias=epst[:, 0:1], scale=1.0 / 4096), r=[b_osm, b_eps], w=[b_osm])
                    S.op("dve", lambda e: e.reciprocal(out=osm[:, 10:11], in_=osm[:, 9:10]), r=[b_osm], w=[b_osm])
                    S.op("dve", lambda e: e.tensor_scalar(out=ych, in0=ych, scalar1=osm[:, 10:11], scalar2=None, op0=ALU.mult), r=[bY, b_osm], w=[bY])
                    for q4 in range(8):
                        pt, pb = bank()
                        for cc in range(4):
                            ch = q4 * 4 + cc
                            tr(pt[:, cc * 128:(cc + 1) * 128], pb, ych[:, ch * 128:(ch + 1) * 128], bY)
                        for cc in range(4):
                            ch = q4 * 4 + cc
                            S.op("dve", lambda e: e.tensor_scalar(out=Yt[:, ch, :], in0=pt[:, cc * 128:(cc + 1) * 128], scalar1=nwc[:, ch:ch + 1], scalar2=None, op0=ALU.mult),
                                 r=[pb, b_nwc, bY], w=[bZ])
                    S.dma("pool", yT_v[:, 0:32, rows], Yt, r=[bZ], w=[byT[c // 4]])
        S.op("dve", lambda e: e.memset(xsflat[:, 0:8], 0.0), r=[], w=[b_xs, bS, bY])
        S.op("dve", lambda e: e.memset(kT2[:, 0:8], 0.0), w=[b_kT2, bSb])
        S.op("dve", lambda e: e.memset(flatA[:, 0:8], 0.0), w=[b_actT, bVd, bVw, bX, bBt, bZ, bCT, bBT])
        for b in range(NBLK):
            mixer_out(b, ssd_wo_b, 32)
        g.cur = 1 - g.cur

    def attn_layer(l):
        lam_init = 0.8 - 0.6 * math.exp(-0.3 * l)
        scale = 128 ** -0.5
        BIG = 30000.0
        load_cols(qkn[:, 0:1], b_qkn, attn_q_norm[0, :], 1)
        load_cols(qkn[:, 1:2], b_qkn, attn_k_norm[0, :], 1)
        load_cols(lamt[:, 0:4], b_lam, attn_lambda[0].rearrange("a d -> (a d)"), 4)
        S.op("dve", lambda e: e.tensor_tensor(out=lamt[:, 4:5], in0=lamt[:, 0:1], in1=lamt[:, 1:2], op=ALU.mult), r=[b_lam], w=[b_lam])
        S.op("dve", lambda e: e.tensor_tensor(out=lamt[:, 5:6], in0=lamt[:, 2:3], in1=lamt[:, 3:4], op=ALU.mult), r=[b_lam], w=[b_lam])
        pt, pb = bank()
        mm(pt[:, 0:2], pb, ones[:], b_ones, lamt[:, 4:6], b_lam, True, True)
        S.op("act", lambda e: e.activation(out=lamt[:, 6:8], in_=pt[:, 0:2], func=AF.Exp), r=[pb], w=[b_lam])
        S.op("dve", lambda e: e.tensor_tensor(out=lamt[:, 4:5], in0=lamt[:, 7:8], in1=lamt[:, 6:7], op=ALU.subtract), r=[b_lam], w=[b_lam])
        S.op("dve", lambda e: e.tensor_scalar(out=lamt[:, 4:5], in0=lamt[:, 4:5], scalar1=-lam_init, scalar2=None, op0=ALU.add), r=[b_lam], w=[b_lam])
        bcast_row(subw[:, :], b_subw, attn_sub_norm[0:1, :], 256)
        S.op("dve", lambda e: e.tensor_scalar(out=subw[:, :], in0=subw[:, :], scalar1=1.0 - lam_init, scalar2=None, op0=ALU.mult), r=[b_subw], w=[b_subw])
        bcast_row(tbb[:, :], b_tbb, rel_bias.rearrange("b h -> (b h)").rearrange("(o n) -> o n", o=1), 256)
        S.op("dve", lambda e: e.memset(bcol[:, 32:33], 0.0), w=[b_bcol])
        S.op("dve", lambda e: e.tensor_scalar(out=bcol[:, 33:34], in0=linkt[:, 0:1], scalar1=-1.0, scalar2=BIG, op0=ALU.add, op1=ALU.mult), r=[b_link], w=[b_bcol])
        S.op("dve", lambda e: e.tensor_copy(out=bcol[:, 0:8], in_=tbb[:, 15 * 8:16 * 8]), r=[b_tbb], w=[b_bcol])
        S.op("dve", lambda e: e.tensor_copy(out=bcol[:, 8:16], in_=tbb[:, 31 * 8:32 * 8]), r=[b_tbb], w=[b_bcol])
        S.op("dve", lambda e: e.tensor_scalar(out=bcol[:, 16:32], in0=bcol[:, 0:16], scalar1=bcol[:, 33:34], scalar2=None, op0=ALU.add), r=[b_bcol], w=[b_bcol])
        ohb = [cst[0], cst[1]]
        for h in range(8):
            for bk in range(32):
                for pi_, (c0, c1) in enumerate(((0, 512), (512, 1024), (1024, 1152))):
                    st, stb_ = ohb[(bk * 3 + pi_) % 2]
                    S.dma("sp", st[:, 0:c1 - c0], oh_in[bk, :, c0:c1], w=[stb_])
                    if bk == 0:
                        S.op("dve", lambda e: e.tensor_scalar(out=Tacc[:, c0:c1], in0=st[:, 0:c1 - c0], scalar1=tbb[:, bk * 8 + h:bk * 8 + h + 1],
                                                              scalar2=None, op0=ALU.mult), r=[stb_, b_tbb], w=[b_Tacc])
                    else:
                        S.op("dve", lambda e: e.scalar_tensor_tensor(out=Tacc[:, c0:c1], in0=st[:, 0:c1 - c0], scalar=tbb[:, bk * 8 + h:bk * 8 + h + 1],
                                                                     in1=Tacc[:, c0:c1], op0=ALU.mult, op1=ALU.add), r=[stb_, b_tbb, b_Tacc], w=[b_Tacc])
            S.op("act", lambda e: e.activation(out=Tt[:, :], in_=Tacc[:, :], func=AF.Exp), r=[b_Tacc], w=[b_Tt])
            S.dma("pool", T_d[h], Tt[:, :], r=[b_Tt], w=[b_Td])

        for b in range(NBLK):
            t0 = b * TB
            norm_block(b, l, 0)
            for p in range(16):
                wt, wtb = load_panel(attn_wqkv_b, p * 256)
                for cc in range(2):
                    f = 2 * p + cc
                    pm, pmb = bank()
                    gemm_group(pm, pmb, wt, wtb, cc * 128, 0, TB)
                    sqt, sqb = sq[f % 2]
                    S.op("act", lambda e: e.activation(out=sqt[:, 0:TB], in_=pm[:, :], func=AF.Square), r=[pmb], w=[sqb])
                    pn, pnb = bank()
                    mm(pn[:, :], pnb, ones128[:], b_ones128, sqt[:, 0:TB], sqb, True, True)
                    S.op("act", lambda e: e.activation(out=rstd[:, 0:TB], in_=pn[:, :], func=AF.Sqrt, bias=epst[:, 0:1], scale=1.0),
                         r=[pnb, b_eps], w=[b_rstd])
                    S.op("dve", lambda e: e.reciprocal(out=rstd[:, 0:TB], in_=rstd[:, 0:TB]), r=[b_rstd], w=[b_rstd])
                    ot_, otb_ = st16[f % 2]
                    wcol = 0 if f < 16 else 1
                    S.op("dve", lambda e: e.scalar_tensor_tensor(out=ot_[:, :], in0=pm[:, :], scalar=qkn[:, wcol:wcol + 1], in1=rstd[:, 0:TB],
                                                                 op0=ALU.mult, op1=ALU.mult), r=[pmb, b_qkn, b_rstd], w=[otb_])
                    if f < 16:
                        S.dma("pool", qT_d[f, :, t0:t0 + TB], ot_[:, :], r=[otb_], w=[b_qT])
                    else:
                        S.dma("pool", kT_d[f - 16, :, t0:t0 + TB], ot_[:, :], r=[otb_], w=[b_kT])
            for h in range(8):
                wt, wtb = load_panel(attn_wqkv_b, 4096 + h * 256)
                for tt in range(4):
                    pv, pvb = bank()
                    for kc in range(KC):
                        mm(pv[:, 0:256], pvb, hT[:, kc, tt * 128:(tt + 1) * 128], b_hT, wt[:, kc, 0:256], wtb, kc == 0, kc == KC - 1)
                    ot_, otb_ = st16[tt % 2]
                    evac(ot_[:, 0:256], otb_, pv[:, 0:256], pvb)
                    S.dma("pool", v_d[t0 + tt * 128:t0 + (tt + 1) * 128, h * 256:(h + 1) * 256], ot_[:, 0:256], r=[otb_], w=[b_vd])

        Vh = actT[:, :, :].rearrange("p a b -> p (a b)")[:, 0:NT * 258].rearrange("p (k e) -> p k e", e=258)
        kTs = (hT[:, :, :].rearrange("p a b -> p (a b)")[:, 0:TOK], kT2[:, 0:TOK])
        kTb = (b_hT, b_kT2)
        NQB = TOK // 512
        for h in range(8):
            S.dma("sp", Vh[:, :, 0:256], v_d.rearrange("(k p) e -> p k e", p=128)[:, :, h * 256:(h + 1) * 256], r=[b_vd], w=[b_actT])
            S.op("dve", lambda e: e.memset(Vh[:, :, 256:257], 1.0), w=[b_actT])
            S.dma("sp", Tt[:, :], T_d[h], r=[b_Td], w=[b_Tt])
            for j in range(2):
                S.dma("sp", kTs[j], kT_d[2 * h + j], r=[b_kT], w=[kTb[j]])
            for qb in range(NQB):
                qslot = (qb * 512) // SL
                for j in range(2):
                    qt_, qtb_ = qblk[j]
                    S.dma("sp", qt_[:, :], qT_d[2 * h + j, :, qb * 512:(qb + 1) * 512], r=[b_qT], w=[qtb_])
                    g.bank_pool = [4, 5, 6, 7]
                    for kt in range(NT):
                        kslot = (kt * 128) // SL
                        cross = kslot != qslot
                        jj = kt - 4 * qb + 1
                        near = 0 <= jj <= 5
                        ps_, psb_ = bank()
                        mm(ps_[:, :], psb_, kTs[j][:, kt * 128:(kt + 1) * 128], kTb[j], qt_[:, :], qtb_, True, True)
                        p_, pb_ = pT[kt % 2]
                        if near:
                            bc = bcol[:, 32:33]
                        else:
                            idx = (0 if jj < 0 else 8) + (16 if cross else 0) + h
                            bc = bcol[:, idx:idx + 1]
                        S.op("act", lambda e: e.activation(out=p_[:, :], in_=ps_[:, :], func=AF.Exp, bias=bc, scale=scale), r=[psb_, b_bcol], w=[pb_])
                        if near:
                            m0 = 640 - jj * 128
                            S.op("dve", lambda e: e.tensor_tensor(out=p_[:, :], in0=p_[:, :], in1=Tt[:, m0:m0 + 512], op=ALU.mult), r=[pb_, b_Tt], w=[pb_])
                            if cross:
                                S.op("dve", lambda e: e.tensor_scalar(out=p_[:, :], in0=p_[:, :], scalar1=linkt[:, 0:1], scalar2=None, op0=ALU.mult),
                                     r=[pb_, b_link], w=[pb_])
                        for qi in range(4):
                            ob, obb = banks[qi]
                            mm(ob[:, 0:257], obb, p_[:, qi * 128:(qi + 1) * 128], pb_, Vh[:, kt, 0:257], b_actT, kt == 0, kt == NT - 1)
                    g.bank_pool = list(range(8))
                    o_, ob_ = Ost[j]
                    for qi in range(4):
                        ob, obb = banks[qi]
                        evac(o_[:, qi, :], ob_, ob[:, 0:257], obb)
                for qi in range(4):
                    o0, o1 = Ost[0][0], Ost[1][0]
                    S.op("dve", lambda e: e.reciprocal(out=osm[:, 0:1], in_=o0[:, qi, 256:257]), r=[Ost[0][1]], w=[b_osm])
                    S.op("dve", lambda e: e.reciprocal(out=osm[:, 1:2], in_=o1[:, qi, 256:257]), r=[Ost[1][1]], w=[b_osm])
                    S.op("dve", lambda e: e.tensor_scalar(out=ot[:, :], in0=o1[:, qi, 0:256], scalar1=osm[:, 1:2], scalar2=lamt[:, 4:5],
                                                          op0=ALU.mult, op1=ALU.mult), r=[Ost[1][1], b_osm, b_lam], w=[b_ot])
                    S.op("dve", lambda e: e.scalar_tensor_tensor(out=ot[:, :], in0=o0[:, qi, 0:256], scalar=osm[:, 0:1], in1=ot[:, :],
                                                                 op0=ALU.mult, op1=ALU.add), r=[Ost[0][1], b_osm, b_ot], w=[b_ot])
                    S.op("act", lambda e: e.activation(out=ot2[:, :], in_=ot[:, :], func=AF.Square), r=[b_ot], w=[b_ot2])
                    S.op("dve", lambda e: e.reduce_sum(out=osm[:, 2:3], in_=ot2[:, :], axis=AX.X), r=[b_ot2], w=[b_osm])
                    S.op("act", lambda e: e.activation(out=osm[:, 3:4], in_=osm[:, 2:3], func=AF.Sqrt, bias=epst[:, 0:1], scale=1.0 / 256),
                         r=[b_osm, b_eps], w=[b_osm])
                    S.op("dve", lambda e: e.reciprocal(out=osm[:, 4:5], in_=osm[:, 3:4]), r=[b_osm], w=[b_osm])
                    S.op("dve", lambda e: e.scalar_tensor_tensor(out=ot2[:, :], in0=ot[:, :], scalar=osm[:, 4:5], in1=subw[:, :],
                                                                 op0=ALU.mult, op1=ALU.mult), r=[b_ot, b_osm, b_subw], w=[b_ot2])
                    pt, pb = bank()
                    for e2 in range(2):
                        tr(pt[:, e2 * 128:(e2 + 1) * 128], pb, ot2[:, e2 * 128:(e2 + 1) * 128], b_ot2)
                    for e2 in range(2):
                        evac(ytb_[:, e2, qi * 128:(qi + 1) * 128], b_ytb, pt[:, e2 * 128:(e2 + 1) * 128], pb)
                S.dma("pool", yT_v[:, 2 * h:2 * h + 2, qb * 512:(qb + 1) * 512], ytb_[:, :, :], r=[b_ytb], w=[byT[qb]])
        for b in range(NBLK):
            mixer_out(b, attn_wo_b, 16)
        g.cur = 1 - g.cur

    for l in layers:
        if mixers and l % 4 == 1 and 1 in MIX:
            attn_layer(l)
        if mixers and l % 4 == 3 and 3 in MIX:
            hgrn_layer(l)
        if mixers and l % 4 == 2 and 2 in MIX:
            mlstm_layer(l)
        if mixers and l % 4 == 0 and 0 in MIX:
            ssd_layer(l)
        if ffn:
            ffn_layer(l)

    yo = [cst[0], cst[1]]
    for b in range(NBLK):
        t0 = b * TB
        S.dma("sp", xs[:, :, 0:TB], xT_v2[g.cur][:, :, t0:t0 + TB], r=[bxT2[g.cur][b]], w=[b_xs])
        for j in range(4):
            for q4 in range(4):
                yt, ytb = yo[(j * 4 + q4) % 2]
                pt, pb = bank()
                for cc in range(4):
                    c = q4 * 4 + cc
                    tr(pt[:, cc * 128:(cc + 1) * 128], pb, xs[:, c, j * 128:(j + 1) * 128], b_xs)
                evac(yt[:, 0:512], ytb, pt[:, :], pb)
                S.dma("pool", y_out[t0 + j * 128:t0 + (j + 1) * 128, q4 * 512:(q4 + 1) * 512], yt[:, 0:512], r=[ytb], w=[Buf("yo")])
    for q in ("pool", "sp"):
        for k in range(S.NDMA):
            if S.dcnt[q][k]:
                S._wait("sp", (f"d_{q}{k}", S.dcnt[q][k], S.dsem[q][k], None))
    g.S = S
    es.close()
    return nc, g


IMPLEMENTED_MIXERS = (0, 1, 2, 3)
NSLOT_FULL, SL_FULL = 4, 2048


def kernel(**inputs):
    xp = np.asarray(inputs["x_prompt"], dtype=np.float32)
    xsm = np.asarray(inputs["x_sample"], dtype=np.float32)
    TOK = NSLOT_FULL * SL_FULL
    ncores = 8
    assign = [[("p", 0)], [("p", 1)]]
    nxt = 0
    for c in range(6):
        n = 3 if c < 4 else 2
        assign.append([("s", nxt + i) for i in range(n)])
        nxt += n
    xs_core, links = [], []
    for c in range(ncores):
        xc = np.zeros((TOK, D), np.float32)
        if assign[c][0][0] == "p":
            xc[:] = xp[assign[c][0][1]]
            links.append(1.0)
        else:
            for i, (_, b) in enumerate(assign[c]):
                xc[i * SL_FULL:(i + 1) * SL_FULL] = xsm[b]
            links.append(0.0)
        xs_core.append(xc)
    nc, g = build(NSLOT_FULL, SL_FULL, layers=(0, 1, 2, 3), mixers=True, ffn=True, impl=IMPLEMENTED_MIXERS)
    shared = {"ident": np.eye(128, dtype=np.float32)}
    for k in ("ln1", "ln2", "ffn_w_in", "ffn_conv_w", "ffn_conv_b", "ffn_w_out"):
        shared[k] = np.ascontiguousarray(np.asarray(inputs[k], dtype=np.float32))
    names = {0: ("ssd_w_in", "ssd_conv_w", "ssd_conv_b", "ssd_dt_bias", "ssd_a_log", "ssd_d", "ssd_norm_w", "ssd_w_out"),
             1: ("attn_w_qkv", "attn_q_norm", "attn_k_norm", "attn_lambda", "attn_sub_norm", "attn_w_out", "rel_bias"),
             2: ("mlstm_w_in", "mlstm_gate_b", "mlstm_norm_w", "mlstm_w_out"),
             3: ("hgrn_w_in", "hgrn_lb", "hgrn_norm_w", "hgrn_w_out")}
    for m in IMPLEMENTED_MIXERS:
        for k in names[m]:
            shared[k] = np.ascontiguousarray(np.asarray(inputs[k], dtype=np.float32))
    if 1 in IMPLEMENTED_MIXERS:
        shared["oh"] = make_oh()
    if set(IMPLEMENTED_MIXERS) & {0, 2, 3}:
        tri = np.triu(np.ones((128, 128), np.float32))
        shared["masks"] = np.ascontiguousarray(np.stack([tri, tri.T]))
    in_maps = []
    for c in range(ncores):
        m = dict(shared)
        m["x"] = xs_core[c]
        m["link"] = np.full((128, 1), links[c], np.float32)
        in_maps.append(m)
    res = run_bass_kernel_spmd(nc, in_maps, core_ids=list(range(ncores)))
    y_prompt = np.zeros_like(xp)
    y_sample = np.zeros_like(xsm)
    for c in range(ncores):
        yc = np.asarray(res.results[c]["y"], dtype=np.float32)
        if assign[c][0][0] == "p":
            y_prompt[assign[c][0][1]] = yc
        else:
            for i, (_, b) in enumerate(assign[c]):
                y_sample[b] = yc[i * SL_FULL:(i + 1) * SL_FULL]
    return (y_prompt, y_sample)


def make_oh():
    import jax
    import jax.numpy as jnp
    half, max_exact = 16, 8
    kk = np.arange(128, dtype=np.int32)[:, None]
    m = np.arange(1152, dtype=np.int32)[None, :] - 512
    cpu = jax.devices('cpu')[0]
    rel = jax.device_put(jnp.asarray(kk - m), cpu)
    ret = jnp.where(rel > 0, half, 0)
    n = jnp.abs(rel)
    nf = jnp.maximum(n, 1).astype(jnp.float32)
    large = max_exact + (jnp.log(nf / max_exact) / math.log(128 / max_exact) * (half - max_exact)).astype(jnp.int32)
    large = jnp.minimum(large, half - 1)
    bucket = np.asarray(ret + jnp.where(n < max_exact, n, large))
    oh = (bucket[None, :, :] == np.arange(32)[:, None, None]).astype(np.float32)
    return np.ascontiguousarray(oh)
```
